# Optimizing a Trainium2 kernel written in Bass

```python
import math
import jax, jax.numpy as jnp
from jax import lax
import numpy as np

D_MODEL = 1024
BATCH = 2
SEQ = 8192
DEPTH = 1

GRID_W = 64
HEAD_DIM = 64
N_Q_HEADS = 8
N_KV_HEADS = 2
GQA_GROUP = N_Q_HEADS // N_KV_HEADS
ATTN_WIDTH = N_Q_HEADS * HEAD_DIM
KV_WIDTH = N_KV_HEADS * HEAD_DIM
Q_BLOCK = 128
AXIAL_DIM = HEAD_DIM // 2
ROPE_THETA = 10000.0
CHUNK = 128
N_GROUPS = 8
GMLP_WIDTH = 512
GROUP_DIM = GMLP_WIDTH // N_GROUPS
N_BRANCHES = 2
EPS = 1e-6

IN_SIZES = (ATTN_WIDTH, KV_WIDTH, KV_WIDTH, ATTN_WIDTH,
            GMLP_WIDTH, GMLP_WIDTH, GMLP_WIDTH, N_BRANCHES * D_MODEL)
IN_WIDTH = sum(IN_SIZES)
IN_SPLITS = tuple(int(s) for s in np.cumsum(IN_SIZES)[:-1])

kernel_name = "hybrid_gqa_axialrope_gmlp_gated_merge"


def rmsnorm(x, gain):
    x32 = x.astype(jnp.float32)
    y = x32 * lax.rsqrt(jnp.mean(x32 * x32, axis=-1, keepdims=True) + EPS)
    return (y * gain.astype(jnp.float32)).astype(x.dtype)


def layernorm(x, gain, bias):
    x32 = x.astype(jnp.float32)
    mu = jnp.mean(x32, axis=-1, keepdims=True)
    var = jnp.mean(jnp.square(x32 - mu), axis=-1, keepdims=True)
    y = (x32 - mu) * lax.rsqrt(var + EPS)
    return (y * gain.astype(jnp.float32) + bias.astype(jnp.float32)).astype(x.dtype)


def rope_half(x, ang):
    n = x.shape[-1] // 2
    x1, x2 = x[..., :n], x[..., n:]
    cos, sin = jnp.cos(ang).astype(x.dtype), jnp.sin(ang).astype(x.dtype)
    return jnp.concatenate([x1 * cos - x2 * sin, x1 * sin + x2 * cos], axis=-1)


def axial_rope(x, row_ang, col_ang):
    return jnp.concatenate([rope_half(x[..., :AXIAL_DIM], row_ang),
                            rope_half(x[..., AXIAL_DIM:], col_ang)], axis=-1)


def block_attention(q, k, v):
    B, S = q.shape[0], q.shape[1]
    n_blk = S // Q_BLOCK
    scale = 1.0 / math.sqrt(HEAD_DIM)
    qb = q.reshape(B, n_blk, Q_BLOCK, N_KV_HEADS, GQA_GROUP, HEAD_DIM).transpose(1, 0, 3, 4, 2, 5)
    kt = k.transpose(0, 2, 1, 3)
    vt = v.transpose(0, 2, 1, 3)

    def one_block(q_blk):
        s = jnp.einsum('bhgqd,bhkd->bhgqk', q_blk, kt).astype(jnp.float32) * scale
        p = jax.nn.softmax(s, axis=-1).astype(vt.dtype)
        return jnp.einsum('bhgqk,bhkd->bhgqd', p, vt)

    o = lax.map(one_block, qb)
    return o.transpose(1, 0, 4, 2, 3, 5).reshape(B, S, ATTN_WIDTH)


def hybrid_layer(x, norm_gain, w_in, q_gain, k_gain, w_proj_attn,
                 ln_v_gain, ln_v_bias, w_spatial, b_spatial, w_proj_gmlp,
                 b_merge, w_out):
    B, S, _ = x.shape
    rows = S // GRID_W
    h = rmsnorm(x, norm_gain)
    z = jnp.einsum('bsd,de->bse', h, w_in)
    q, k, v, gate_a, u_in, v_in, gate_b, g_merge = jnp.split(z, IN_SPLITS, axis=-1)

    row_idx = jnp.repeat(jnp.arange(rows, dtype=jnp.float32), GRID_W)
    col_idx = jnp.tile(jnp.arange(GRID_W, dtype=jnp.float32), rows)
    inv_freq = ROPE_THETA ** (-jnp.arange(0, AXIAL_DIM, 2, dtype=jnp.float32) / AXIAL_DIM)
    row_ang = (row_idx[:, None] * inv_freq)[:, None, :]
    col_ang = (col_idx[:, None] * inv_freq)[:, None, :]
    q = axial_rope(rmsnorm(q.reshape(B, S, N_Q_HEADS, HEAD_DIM), q_gain), row_ang, col_ang)
    k = axial_rope(rmsnorm(k.reshape(B, S, N_KV_HEADS, HEAD_DIM), k_gain), row_ang, col_ang)
    v = v.reshape(B, S, N_KV_HEADS, HEAD_DIM)
    attn = block_attention(q, k, v) * jax.nn.silu(gate_a)
    y_a = jnp.einsum('bse,ed->bsd', attn, w_proj_attn)

    u = jax.nn.gelu(u_in)
    vv = layernorm(jax.nn.gelu(v_in), ln_v_gain, ln_v_bias)
    vv = vv.reshape(B, S // CHUNK, CHUNK, N_GROUPS, GROUP_DIM)
    v_mix = jnp.einsum('gij,bcjgd->bcigd', w_spatial, vv) + b_spatial.T[:, :, None]
    gm = u * v_mix.reshape(B, S, GMLP_WIDTH) * jax.nn.silu(gate_b)
    y_b = jnp.einsum('bse,ed->bsd', gm, w_proj_gmlp)

    g = jax.nn.sigmoid(g_merge.reshape(B, S, N_BRANCHES, D_MODEL) + b_merge)
    y = g[:, :, 0, :] * y_a + g[:, :, 1, :] * y_b
    return x + jnp.einsum('bsd,de->bse', y, w_out)


def setup_inputs(seed: int = 0) -> dict:
    key = jax.random.key(seed)
    ks = jax.random.split(key, 16)
    L = DEPTH
    nrm = lambda k, shape, fan: jax.random.normal(k, shape, jnp.float32) * (fan ** -0.5)
    return {
        "x": jax.random.normal(ks[0], (BATCH, SEQ, D_MODEL), jnp.float32),
        "norm_gain": 1.0 + 0.05 * jax.random.normal(ks[1], (L, D_MODEL), jnp.float32),
        "w_in": nrm(ks[2], (L, D_MODEL, IN_WIDTH), D_MODEL),
        "q_gain": 1.0 + 0.05 * jax.random.normal(ks[3], (L, HEAD_DIM), jnp.float32),
        "k_gain": 1.0 + 0.05 * jax.random.normal(ks[4], (L, HEAD_DIM), jnp.float32),
        "w_proj_attn": nrm(ks[5], (L, ATTN_WIDTH, D_MODEL), ATTN_WIDTH),
        "ln_v_gain": 1.0 + 0.05 * jax.random.normal(ks[6], (L, GMLP_WIDTH), jnp.float32),
        "ln_v_bias": 0.02 * jax.random.normal(ks[7], (L, GMLP_WIDTH), jnp.float32),
        "w_spatial": nrm(ks[8], (L, N_GROUPS, CHUNK, CHUNK), CHUNK),
        "b_spatial": 1.0 + 0.05 * jax.random.normal(ks[9], (L, N_GROUPS, CHUNK), jnp.float32),
        "w_proj_gmlp": nrm(ks[10], (L, GMLP_WIDTH, D_MODEL), GMLP_WIDTH),
        "b_merge": 0.02 * jax.random.normal(ks[11], (L, N_BRANCHES, D_MODEL), jnp.float32),
        "w_out": nrm(ks[12], (L, D_MODEL, D_MODEL), D_MODEL),
        "final_gain": 1.0 + 0.05 * jax.random.normal(ks[13], (D_MODEL,), jnp.float32),
    }


def reference(x, norm_gain, w_in, q_gain, k_gain, w_proj_attn, ln_v_gain, ln_v_bias,
              w_spatial, b_spatial, w_proj_gmlp, b_merge, w_out, final_gain):
    for l in range(DEPTH):
        x = hybrid_layer(x, norm_gain[l], w_in[l], q_gain[l], k_gain[l], w_proj_attn[l],
                         ln_v_gain[l], ln_v_bias[l], w_spatial[l], b_spatial[l],
                         w_proj_gmlp[l], b_merge[l], w_out[l])
    return rmsnorm(x, final_gain)
```

```python
import contextlib
import numpy as np
import concourse.bass as bass
import concourse.mybir as mybir
from concourse.bass_utils import run_bass_kernel_spmd

F32 = mybir.dt.float32
BF16 = mybir.dt.bfloat16
AF = mybir.ActivationFunctionType
ALU = mybir.AluOpType

P = 128
D = 1024
S_ALL = 8192
S_OWN = 2048
NG_ALL = 16
NG_OWN = 4
G = 512
EPS = 1e-6
ENGS = ('pe', 'act', 'dve', 'pool', 'sp')


class Sched:
    DEF_T = dict(pe=0.25, act=0.7, dve=0.7, pool=1.3, sp=0.05)

    def __init__(self):
        self.ops = []
        self.lastw = {}
        self.readers = {}
        self.marks = []
        self.reorder = True

    def op(self, eng, fn, r=(), w=(), dma=None, t=None, inc=16):
        i = len(self.ops)
        raw, other = set(), set()
        for b in r:
            lw = self.lastw.get(b)
            if lw is not None:
                raw.add(lw)
        for b in w:
            lw = self.lastw.get(b)
            if lw is not None:
                other.add(lw)
            other.update(self.readers.get(b, ()))
        for b in r:
            self.readers.setdefault(b, []).append(i)
        for b in w:
            self.lastw[b] = i
            self.readers[b] = []
        raw.discard(i)
        other.discard(i)
        if t is None:
            t = self.DEF_T[eng]
        self.ops.append(dict(eng=eng, fn=fn, raw=raw, other=other - raw, dma=dma, t=t, inc=inc))
        return i

    def barrier(self):
        self.marks.append((len(self.ops), self.reorder))

    def _schedule(self, idxs):
        ops = self.ops
        iset = set(idxs)
        preds = {i: set(d for d in (ops[i]['raw'] | ops[i]['other']) if d in iset) for i in idxs}
        last_stream = {}
        for i in idxs:
            k = ops[i]['dma']
            if k is not None:
                if k in last_stream:
                    preds[i].add(last_stream[k])
                last_stream[k] = i
        succs = {i: [] for i in idxs}
        indeg = {}
        for i in idxs:
            indeg[i] = len(preds[i])
            for d in preds[i]:
                succs[d].append(i)
        HOP = 0.22
        rtime = {i: 0.0 for i in idxs}
        ready = {e: [] for e in ENGS}
        for i in idxs:
            if indeg[i] == 0:
                ready[ops[i]['eng']].append(i)
        free = {e: 0.0 for e in ENGS}
        order = []
        n = len(idxs)
        while len(order) < n:
            best = None
            for e in ENGS:
                for i in ready[e]:
                    key = (max(rtime[i], free[e]), i)
                    if best is None or key < best[0]:
                        best = (key, i, e)
            (st_, _), i, e = best
            ready[e].remove(i)
            o = ops[i]
            if o['dma'] is not None:
                free[e] = st_ + (0.05 if e == 'sp' else 0.6)
                fin = st_ + 2.0 + o['t']
            else:
                free[e] = st_ + o['t']
                fin = free[e]
            order.append(i)
            for j in succs[i]:
                rtime[j] = max(rtime[j], fin + HOP)
                indeg[j] -= 1
                if indeg[j] == 0:
                    ready[ops[j]['eng']].append(j)
        return order

    def emit(self, nc, stack):
        ops = self.ops
        seq = []
        start = 0
        for (m, reorder) in self.marks:
            idxs = list(range(start, m))
            seq += [('op', i) for i in (self._schedule(idxs) if reorder else idxs)]
            seq.append(('bar',))
            start = m
        assert start == len(ops), "program must end with a barrier"
        for i, o in enumerate(ops):
            deps = set()
            for d in (o['raw'] | o['other']):
                p = ops[d]
                if p['dma'] is None and o['dma'] is None and p['eng'] == o['eng'] and o['eng'] == 'pe':
                    continue
                deps.add(d)
            o['deps'] = deps
        has_dep = [False] * len(ops)
        for o in ops:
            for d in o['deps']:
                has_dep[d] = True
        last = {}
        final = []
        for ent in seq:
            if ent[0] == 'op':
                i = ent[1]
                o = ops[i]
                k = ('dma', o['dma']) if o['dma'] is not None else ('eng', o['eng'])
                last[k] = i
                final.append(ent)
            else:
                dl = set(last.values())
                for d in dl:
                    has_dep[d] = True
                for e in ENGS:
                    final.append(('wait', e, dl))
        eng_sem, eng_cnt, dma_sem, dma_cnt = {}, {}, {}, {}
        for ent in final:
            if ent[0] != 'op':
                continue
            i = ent[1]
            o = ops[i]
            o['sig'] = None
            if o['dma'] is not None:
                k = o['dma']
                if k not in dma_sem:
                    dma_sem[k] = stack.enter_context(nc.semaphore("d%d" % len(dma_sem)))
                    dma_cnt[k] = 0
                dma_cnt[k] += o['inc']
                o['sig'] = (dma_sem[k], dma_cnt[k])
            else:
                e = o['eng']
                if e not in eng_sem:
                    eng_sem[e] = stack.enter_context(nc.semaphore("e_" + e))
                    eng_cnt[e] = 0
                if has_dep[i]:
                    eng_cnt[e] += 1
                    o['sig'] = (eng_sem[e], eng_cnt[e])
        self.stats = dict(n_ops=len(ops), sems=len(eng_sem) + len(dma_sem), eng_cnt=dict(eng_cnt))
        block = stack.enter_context(nc.Block())

        def run(ename, eng):
            known = {}

            def waits(depset):
                need = {}
                for d in depset:
                    sig = ops[d]['sig']
                    assert sig is not None
                    s, v = sig
                    if id(s) not in need or need[id(s)][1] < v:
                        need[id(s)] = (s, v)
                for key, (s, v) in need.items():
                    if known.get(key, 0) >= v:
                        continue
                    eng.wait_ge(s, v)
                    known[key] = v

            for ent in final:
                if ent[0] == 'wait':
                    if ent[1] == ename:
                        waits(ent[2])
                    continue
                o = ops[ent[1]]
                if o['eng'] != ename:
                    continue
                waits(o['deps'])
                ins = o['fn'](eng)
                if o['sig'] is not None:
                    ins.then_inc(o['sig'][0], o['inc'] if o['dma'] is not None else 1)

        @block.tensor
        def _(e):
            run('pe', e)

        @block.scalar
        def _(e):
            run('act', e)

        @block.vector
        def _(e):
            run('dve', e)

        @block.gpsimd
        def _(e):
            run('pool', e)

        @block.sync
        def _(e):
            run('sp', e)


class Rot:
    def __init__(self, name, aps):
        self.name, self.aps, self.i = name, aps, 0

    def next(self):
        k = self.i % len(self.aps)
        self.i += 1
        return self.aps[k], (self.name, k)


class Bump:
    def __init__(self, arena, start, end):
        self.arena, self.off, self.end = arena, start, end

    def alloc(self, nbytes, dt, shape=None):
        nbytes = (nbytes + 63) // 64 * 64
        assert self.off + nbytes <= self.end, (self.off, nbytes, self.end)
        ap = self.arena[:, self.off // 2:(self.off + nbytes) // 2]
        self.off += nbytes
        if dt != BF16:
            ap = ap.bitcast(dt)
        return ap


KB = 1024
OFF_KT, OFF_V, OFF_QT, OFF_HT, OFF_GM, OFF_ATTN, OFF_FLEX, ARENA = (
    0, 16 * KB, 40 * KB, 56 * KB, 88 * KB, 104 * KB, 120 * KB, 200 * KB)


def build_program(debug=False, max_phase=4, n_a_groups=NG_OWN):
    nc = bass.Bass("TRN2", target_bir_lowering=False)

    def din(name, shape, dt=F32):
        return nc.dram_tensor(name, shape, dt, kind="ExternalInput").ap()

    x = din("x", [S_OWN, D])
    ctab = din("ctab", [P, S_OWN])
    stab = din("stab", [P, S_OWN])
    cc_srcK = nc.dram_tensor("cc_srcK", [P, S_OWN], BF16).ap()
    cc_dstK = nc.dram_tensor("cc_dstK", [4 * P, S_OWN], BF16).ap()
    cc_srcV = nc.dram_tensor("cc_srcV", [P, 16 * 192], BF16).ap()
    cc_dstV = nc.dram_tensor("cc_dstV", [4 * P, 16 * 192], BF16).ap()
    w_kvq = din("w_kvq", [P, 8, 768])
    w_a2 = din("w_a2", [P, 8, 1536])
    w_ga = din("w_ga", [P, 8, 512])
    w_g = din("w_g", [P, 8, 2048])
    w_pa = din("w_pa", [P, 4, 1024])
    w_pg = din("w_pg", [P, 4, 1024])
    w_o = din("w_o", [P, 8, 1024])
    wsT = din("wsT", [P, 8, 128])
    ngv = din("ng", [D])
    ngtd = din("ng_t", [P, 8])
    gqk = din("gqk", [P, 2])
    gq_row = din("gq_row", [64])
    gk_row = din("gk_row", [64])
    lngv = din("lng", [512])
    lnbv = din("lnb", [512])
    bspd = din("bsp", [P, 4, 512])
    bmd = din("bm", [P, 16])
    fgv = din("fg", [D])
    identd = din("ident", [P, P])
    swmd = din("swm", [P, P])
    bod = din("bo", [P, P])
    out = nc.dram_tensor("out", [S_OWN, D], F32, kind="ExternalOutput").ap()
    dbg = {}
    if debug:
        for nm, shp, dt in [("d_kt", [P, S_ALL], BF16), ("d_v", [P, 64 * 192], BF16), ("d_qt", [P, 4 * S_OWN], BF16),
                            ("d_ht", [P, 4 * 8 * 512], BF16), ("d_gm", [P, 4 * S_OWN], BF16),
                            ("d_attn", [P, 4 * S_OWN], BF16)]:
            dbg[nm] = nc.dram_tensor(nm, shp, dt, kind="ExternalOutput").ap()

    S = Sched()
    with contextlib.ExitStack() as st:
        arena = st.enter_context(nc.sbuf_tensor("arena", [P, ARENA // 2], BF16))

        def sb(name, shape, dt=F32):
            return st.enter_context(nc.sbuf_tensor("s_" + name, shape, dt))

        PS = [st.enter_context(nc.psum_tensor("ps%d" % i, [P, 1024], F32)) for i in range(4)]

        def bank(b):
            return PS[b // 2][:, (b % 2) * 512:(b % 2) * 512 + 512]

        def bkey(b):
            return ('ps', b)

        ident = sb("ident", [P, P], BF16)
        swm = sb("swm", [P, P], BF16)
        bo = sb("bo", [P, P], BF16)
        gqk_t = sb("gqk", [P, 2], F32)
        bm_t = sb("bm", [P, 16], F32)
        negh = sb("negh", [P, 1], F32)
        ngT = sb("ngT", [P, 8], F32)
        gq_b = sb("gq_b", [P, 64], F32)
        gk_b = sb("gk_b", [P, 64], F32)
        mqk = sb("mqk", [P, 2], F32)
        nbias = sb("nbias", [P, 1], F32)
        ssall = sb("ssall", [P, 64], F32)
        msall = sb("msall", [P, 64], F32)
        rsall = sb("rsall", [P, 64], F32)
        fss = sb("fss", [P, 16], F32)
        fms = sb("fms", [P, 16], F32)
        frs = sb("frs", [P, 16], F32)
        lnst = sb("lnst", [P, 16 * 6], F32)
        lnmv = sb("lnmv", [P, 16 * 2], F32)
        lnve = sb("lnve", [P, 16], F32)
        lnrs = sb("lnrs", [P, 16], F32)

        KT = arena[:, OFF_KT // 2:(OFF_KT + 16 * KB) // 2]
        Vsb = arena[:, OFF_V // 2:(OFF_V + 24 * KB) // 2].rearrange("p (k c) -> p k c", c=192)
        QT = arena[:, OFF_QT // 2:(OFF_QT + 16 * KB) // 2].rearrange("p (j t) -> p j t", j=4)
        HT = arena[:, OFF_HT // 2:(OFF_HT + 32 * KB) // 2].rearrange("p (g k t) -> p g k t", g=4, k=8)
        GM = arena[:, OFF_GM // 2:(OFF_GM + 16 * KB) // 2].rearrange("p (c t) -> p c t", c=4)
        ATT = arena[:, OFF_ATTN // 2:(OFF_ATTN + 16 * KB) // 2].rearrange("p (j t) -> p j t", j=4)

        def dma(eng, out_ap, in_ap, r, w, stream):
            S.op(eng, lambda e: e.dma_start(out=out_ap, in_=in_ap), r=r, w=w, dma=stream)

        dma('pool', ident[:], identd, [], ['ident'], 'c_ident')
        dma('pool', swm[:], swmd, [], ['swm'], 'c_swm')
        dma('pool', bo[:], bod, [], ['bo'], 'c_bo')
        dma('sp', gqk_t[:], gqk, [], ['gqk'], 'c_gqk')
        dma('sp', bm_t[:], bmd, [], ['bm'], 'c_bm')
        dma('sp', ngT[:], ngtd, [], ['ngT'], 'c_ngT')

        def fold_gain(w_ap, kcs, keys, eng='dve'):
            for kc in kcs:
                S.op(eng, lambda e, kc=kc: e.tensor_scalar(out=w_ap[:, kc, :], in0=w_ap[:, kc, :], scalar1=ngT[:, kc:kc + 1], scalar2=None, op0=ALU.mult),
                     r=list(keys) + ['ngT'], w=list(keys), t=0.3)
        S.op('pool', lambda e: e.memset(negh[:], -0.5), w=['negh'])
        dma('sp', gq_b[:], gq_row.partition_broadcast(P), [], ['gq_b'], 'c_gqb')
        dma('sp', gk_b[:], gk_row.partition_broadcast(P), [], ['gk_b'], 'c_gkb')
        S.op('dve', lambda e: e.reduce_max(out=mqk[:, 0:1], in_=gq_b[:], axis=mybir.AxisListType.X, apply_absolute_value=True), r=['gq_b'], w=['mq'], t=0.1)
        S.op('dve', lambda e: e.reduce_max(out=mqk[:, 1:2], in_=gk_b[:], axis=mybir.AxisListType.X, apply_absolute_value=True), r=['gk_b'], w=['mk'], t=0.1)
        S.op('dve', lambda e: e.tensor_scalar(out=nbias[:], in0=mqk[:, 0:1], scalar1=mqk[:, 1:2], scalar2=-8.0, op0=ALU.mult, op1=ALU.mult),
             r=['mq', 'mk'], w=['nbias'], t=0.1)
        S.op('pool', lambda e: e.memset(Vsb[:, 0:16, 64:128], 1.0), w=['vones'], t=1.5)

        WA_END = OFF_FLEX + 56 * KB
        WA = Bump(arena, OFF_GM, WA_END)
        wa2 = arena[:, WA_END // 2:ARENA // 2].rearrange("p (k n) -> p k n", k=8)
        wkvq = WA.alloc(12 * KB, BF16).rearrange("p (k n) -> p k n", k=8)
        XT = Rot('xt', [WA.alloc(4 * KB, F32) for _ in range(4)])
        HB = Rot('hb', [WA.alloc(8 * KB, BF16).rearrange("p (t d) -> p t d", t=4) for _ in range(3)])
        CT = Rot('ct', [WA.alloc(2 * KB, F32) for _ in range(3)])
        STb = Rot('st', [WA.alloc(2 * KB, F32) for _ in range(3)])
        HTG = None
        SQ = Rot('sq', [WA.alloc(1 * KB, BF16) for _ in range(2)])
        KGB = Rot('kgb', [WA.alloc(1 * KB, BF16) for _ in range(2)])
        T1 = Rot('t1', [WA.alloc(2 * KB, F32) for _ in range(3)])
        T2 = Rot('t2', [WA.alloc(2 * KB, F32) for _ in range(2)])
        SR = Rot('sr', [WA.alloc(2 * KB, F32) for _ in range(2)])
        RI = Rot('ri', [WA.alloc(2 * KB, F32) for _ in range(2)])

        WKV_KEYS = [('wkvq', 'kv', 0), ('wkvq', 'kv', 1)]
        WQ_KEYS = [('wkvq', 'q', 0), ('wkvq', 'q', 1)]

        def issue_wkvq():
            xkeys = [('xt', k) for k in range(4)]
            for (nm, c0, c1) in (('kv', 0, 256), ('q', 256, 768)):
                for kh in range(2):
                    dma('pool', wkvq[:, kh * 4:(kh + 1) * 4, c0:c1], w_kvq[:, kh * 4:(kh + 1) * 4, c0:c1], xkeys, [('wkvq', nm, kh)], ('wkvq', nm, kh))
                    fold_gain(wkvq[:, :, c0:c1], range(kh * 4, (kh + 1) * 4), [('wkvq', nm, kh)])

        PROJ_B = [2, 3, 4]
        proj_i = [0]
        evac_flip = [0]
        rope_q = []

        def rope_r1a(ps_ap, ps_key, gain_col, dst_ap, dst_key, ct_ap, ct_key, st_ap, st_key):
            c = dict(dst_ap=dst_ap, dst_key=dst_key, st_ap=st_ap, st_key=st_key)
            c['sq'], c['ksq'] = SQ.next()
            c['kgb'], c['kkgb'] = KGB.next()
            c['t1'], c['kt1'] = T1.next()
            sq, kgb, t1 = c['sq'], c['kgb'], c['t1']
            S.op('act', lambda e: e.activation(out=sq, in_=ps_ap, func=AF.Square), r=[ps_key], w=[c['ksq'], ps_key])
            S.op('act', lambda e: e.activation(out=kgb, in_=ps_ap, func=AF.Copy, scale=gain_col), r=[ps_key, 'gqk'], w=[c['kkgb'], ps_key])
            S.op('dve', lambda e: e.scalar_tensor_tensor(out=t1, in0=ps_ap, scalar=gain_col, in1=ct_ap, op0=ALU.mult, op1=ALU.mult),
                 r=[ps_key, 'gqk', ct_key], w=[c['kt1'], ps_key])
            return c

        def rope_r1b(c):
            sq, kgb = c['sq'], c['kgb']
            S.op('pe', lambda e: e.matmul(bank(5), lhsT=swm[:], rhs=kgb, start=True, stop=True), r=['swm', c['kkgb']], w=[bkey(5)], t=0.4)
            S.op('pe', lambda e: e.matmul(bank(6), lhsT=bo[:], rhs=sq, start=True, stop=True), r=['bo', c['ksq']], w=[bkey(6)], t=0.3)

        def rope_r2(c):
            c['t2'], c['kt2'] = T2.next()
            c['sr'], c['ksr'] = SR.next()
            t1, t2, sr, st_ap = c['t1'], c['t2'], c['sr'], c['st_ap']
            S.op('dve', lambda e: e.tensor_tensor(out=t2, in0=bank(5), in1=st_ap, op=ALU.mult), r=[bkey(5), c['st_key']], w=[c['kt2']])
            S.op('act', lambda e: e.activation(out=sr, in_=bank(6), func=AF.Ln, scale=1.0 / 64.0, bias=EPS), r=[bkey(6)], w=[c['ksr']])
            S.op('dve', lambda e: e.tensor_tensor(out=t1, in0=t1, in1=t2, op=ALU.add), r=[c['kt1'], c['kt2']], w=[c['kt1']])

        def rope_r3(c):
            c['ri'], c['kri'] = RI.next()
            t1, sr, ri, dst_ap = c['t1'], c['sr'], c['ri'], c['dst_ap']
            S.op('act', lambda e: e.activation(out=ri, in_=sr, func=AF.Exp, scale=-0.5), r=[c['ksr']], w=[c['kri']])
            S.op('dve', lambda e: e.tensor_tensor(out=dst_ap, in0=t1, in1=ri, op=ALU.mult), r=[c['kt1'], c['kri']], w=[c['dst_key']])

        def rope_push(*args):
            c = rope_r1a(*args)
            if len(rope_q) >= 1:
                rope_r2(rope_q[-1])
            if len(rope_q) >= 2:
                rope_r3(rope_q[-2])
            rope_r1b(c)
            rope_q.append(c)

        def rope_flush():
            if len(rope_q) >= 1:
                rope_r2(rope_q[-1])
            if len(rope_q) >= 2:
                rope_r3(rope_q[-2])
            if len(rope_q) >= 1:
                rope_r3(rope_q[-1])

        def next_proj():
            b = PROJ_B[proj_i[0] % 3]
            proj_i[0] += 1
            return bank(b), bkey(b)

        ginfo = {}

        def stage_a1(g):
            hb, khb = HB.next()
            ginfo[g] = dict(hb=hb, khb=khb)
            for tt in range(4):
                c = g * 4 + tt
                xt, kx = XT.next()
                dma('sp', xt, x[c * P:(c + 1) * P, :], [], [kx], kx)
                S.op('act', lambda e, xt=xt, c=c, hb=hb, tt=tt: e.activation(out=hb[:, tt, :], in_=xt, func=AF.Square, accum_out=ssall[:, c:c + 1]),
                     r=[kx], w=[(khb, tt), ('ss', c)], t=1.0)
                S.op('dve', lambda e, c=c: e.tensor_scalar(out=msall[:, c:c + 1], in0=ssall[:, c:c + 1], scalar1=1.0 / D, scalar2=EPS,
                                                          op0=ALU.mult, op1=ALU.add), r=[('ss', c)], w=[('ms', c)], t=0.08)
                S.op('pool', lambda e, c=c: e.tensor_tensor(out=rsall[:, c:c + 1], in0=msall[:, c:c + 1], in1=negh[:, 0:1], op=ALU.pow),
                     r=[('ms', c), 'negh'], w=[('rs', c)], t=0.5)
                S.op('dve', lambda e, xt=xt, c=c, tt=tt, hb=hb: e.tensor_scalar(out=hb[:, tt, :], in0=xt, scalar1=rsall[:, c:c + 1], scalar2=None, op0=ALU.mult),
                     r=[kx, ('rs', c)], w=[(khb, tt)], t=0.65)

        def stage_a2(g):
            gi = ginfo[g]
            hb, khb = gi['hb'], gi['khb']
            ct, kct = CT.next()
            stt_, kst = STb.next()
            gi.update(ct=ct, kct=kct, st=stt_, kst=kst)
            dma('sp', ct, ctab[:, g * G:(g + 1) * G], [], [kct], kct)
            dma('sp', stt_, stab[:, g * G:(g + 1) * G], [], [kst], kst)
            if g < NG_OWN:
                htg, khtg = HT[:, g], ('HT', g)
            else:
                htg, khtg = HTG.next()
            gi['htg'], gi['khtg'] = htg, khtg
            for kp in range(4):
                b = kp % 2
                pb = bank(b).bitcast(BF16)
                for k2 in range(2):
                    kc = kp * 2 + k2
                    for tt in range(4):
                        S.op('pe', lambda e, pb=pb, k2=k2, tt=tt, kc=kc, hb=hb: e.transpose(
                            out=pb[:, k2 * 512 + tt * 128:k2 * 512 + (tt + 1) * 128], in_=hb[:, tt, kc * 128:(kc + 1) * 128], identity=ident[:]),
                            r=[(khb, tt), 'ident'], w=[bkey(b)], t=0.09)
                dst = htg[:, kp * 2:kp * 2 + 2, :]
                src = pb.rearrange("p (k t) -> p k t", k=2)
                if evac_flip[0] % 4 != 3:
                    S.op('act', lambda e, dst=dst, src=src: e.activation(out=dst, in_=src, func=AF.Copy), r=[bkey(b)], w=[(khtg, kp)])
                else:
                    S.op('dve', lambda e, dst=dst, src=src: e.tensor_copy(out=dst, in_=src), r=[bkey(b)], w=[(khtg, kp)])
                evac_flip[0] += 1

        def stage_a3(g):
            gi = ginfo[g]
            htg, khtg, ct, kct, stt_, kst = gi['htg'], gi['khtg'], gi['ct'], gi['kct'], gi['st'], gi['kst']
            hkeys = [(khtg, kp) for kp in range(4)]
            pk, kbk = next_proj()
            for kc in range(8):
                S.op('pe', lambda e, pk=pk, kc=kc: e.matmul(pk, lhsT=wkvq[:, kc, 0:128], rhs=htg[:, kc, :], start=(kc == 0), stop=(kc == 7)),
                     r=hkeys + WKV_KEYS, w=[kbk])
            rope_push(pk, kbk, gqk_t[:, 1:2], KT[:, g * G:(g + 1) * G], ('KT', g), ct, kct, stt_, kst)
            pv, kbv = next_proj()
            for tt in range(4):
                for kc in range(8):
                    S.op('pe', lambda e, tt=tt, kc=kc, pv=pv: e.matmul(pv[:, tt * 128:(tt + 1) * 128], lhsT=htg[:, kc, tt * 128:(tt + 1) * 128],
                                                                    rhs=wkvq[:, kc, 128:256], start=(kc == 0), stop=(kc == 7)),
                         r=hkeys + WKV_KEYS, w=[kbv], t=0.12)
            vsrc = pv.rearrange("p (t h d) -> p t h d", t=4, h=2)
            S.op('dve', lambda e: e.tensor_copy(out=Vsb[:, g * 4:(g + 1) * 4, 0:64], in_=vsrc[:, :, 0, :]), r=[kbv], w=[('V0', g)])
            S.op('dve', lambda e: e.tensor_copy(out=Vsb[:, g * 4:(g + 1) * 4, 128:192], in_=vsrc[:, :, 1, :]), r=[kbv], w=[('V1', g)])
            if g < NG_OWN:
                for j in range(4):
                    pq, qbk = next_proj()
                    for kc in range(8):
                        S.op('pe', lambda e, pq=pq, kc=kc, j=j: e.matmul(pq, lhsT=wkvq[:, kc, 256 + j * 128:256 + (j + 1) * 128], rhs=htg[:, kc, :],
                                                                      start=(kc == 0), stop=(kc == 7)),
                             r=hkeys + WQ_KEYS, w=[qbk])
                    rope_push(pq, qbk, gqk_t[:, 0:1], QT[:, j, g * G:(g + 1) * G], ('QT', j, g), ct, kct, stt_, kst)

        for s_ in range(n_a_groups + 2):
            if s_ < n_a_groups:
                stage_a1(s_)
            if s_ == 0:
                issue_wkvq()
            if 0 <= s_ - 1 < n_a_groups:
                stage_a2(s_ - 1)
            if 0 <= s_ - 2 < n_a_groups:
                stage_a3(s_ - 2)
        rope_flush()
        for cb in (1, 0, 2):
            for kh in range(2):
                dma('pool', wa2[:, kh * 4:(kh + 1) * 4, cb * 512:(cb + 1) * 512], w_a2[:, kh * 4:(kh + 1) * 4, cb * 512:(cb + 1) * 512], [('KT', 0)],
                    [('wa2', cb, kh)], ('wa2', cb, kh))
                fold_gain(wa2[:, :, cb * 512:(cb + 1) * 512], range(kh * 4, (kh + 1) * 4), [('wa2', cb, kh)])

        if debug:
            S.barrier()
            dma('sp', dbg["d_kt"], KT, [], ['dbg1'], 'dbg1')
            dma('sp', dbg["d_v"], arena[:, OFF_V // 2:(OFF_V + 24 * KB) // 2], [], ['dbg2'], 'dbg2')
            dma('sp', dbg["d_qt"], arena[:, OFF_QT // 2:(OFF_QT + 16 * KB) // 2], [], ['dbg3'], 'dbg3')
            dma('sp', dbg["d_ht"], arena[:, OFF_HT // 2:(OFF_HT + 32 * KB) // 2], [], ['dbg4'], 'dbg4')
        S.barrier()

        GROUPS = [[0, 1, 2, 3], [4, 5, 6, 7]]
        dma('sp', cc_srcK, KT[:, 0:S_OWN], [('KT', g) for g in range(4)], ['ccsK'], 'ccsK')
        dma('sp', cc_srcV, Vsb[:, 0:16, :], [('V0', g) for g in range(4)] + [('V1', g) for g in range(4)] + ['vones'], ['ccsV'], 'ccsV')
        S.op('pool', lambda e: e.collective_compute("AllGather", ALU.bypass, replica_groups=GROUPS, ins=[cc_srcK.opt()], outs=[cc_dstK.opt()]),
             r=['ccsK'], w=['ccdK'], dma='ccK', inc=1, t=45.0)
        S.op('pool', lambda e: e.collective_compute("AllGather", ALU.bypass, replica_groups=GROUPS, ins=[cc_srcV.opt()], outs=[cc_dstV.opt()]),
             r=['ccsV'], w=['ccdV'], dma='ccV', inc=1, t=65.0)
        for rk in range(4):
            dma('sp', KT[:, rk * S_OWN:(rk + 1) * S_OWN], cc_dstK[rk * P:(rk + 1) * P, :], ['ccdK', 'ccsK'],
                [('KT', rk * 4 + g) for g in range(4)], ('ldK', rk))
        for rk in range(4):
            dma('sp', Vsb[:, rk * 16:(rk + 1) * 16, :], cc_dstV[rk * P:(rk + 1) * P, :].rearrange("p (k c) -> p k c", c=192), ['ccdV', 'ccsV'],
                [('V0', rk * 4 + g) for g in range(4)] + [('V1', rk * 4 + g) for g in range(4)] + ['vones'], ('ldV', rk))

        NGB = min(NG_OWN, n_a_groups) if max_phase >= 2 else 0
        WB = Bump(arena, OFF_ATTN, WA_END)
        wst = WB.alloc(2 * KB, BF16).rearrange("p (g i) -> p g i", g=8)
        lng = WB.alloc(2 * KB, F32)
        lnb = WB.alloc(2 * KB, F32)
        bsp = WB.alloc(8 * KB, F32).rearrange("p (c t) -> p c t", c=4)
        GV = Rot('gv', [WB.alloc(2 * KB, F32) for _ in range(2)])
        VN = Rot('vn', [WB.alloc(2 * KB, F32) for _ in range(2)])
        VTOK = Rot('vtok', [WB.alloc(4 * KB, BF16).rearrange("p (t c) -> p t c", t=4) for _ in range(2)])
        USB = Rot('usb', [WB.alloc(8 * KB, F32).rearrange("p (c t) -> p c t", c=4) for _ in range(2)])
        SGB = Rot('sgb', [WB.alloc(2 * KB, F32) for _ in range(2)])
        TA = Rot('ta', [WB.alloc(2 * KB, F32) for _ in range(2)])
        TB = Rot('tb', [WB.alloc(2 * KB, F32) for _ in range(2)])
        WA2_U = [('wa2', 0, 0), ('wa2', 0, 1)]
        WA2_V = [('wa2', 1, 0), ('wa2', 1, 1)]
        WA2_G = [('wa2', 2, 0), ('wa2', 2, 1)]
        dma('pool', wst, wsT, [], ['wst'], 'c_wst')
        dma('sp', lng, lngv.partition_broadcast(P), [], ['lng'], 'c_lng')
        dma('sp', lnb, lnbv.partition_broadcast(P), [], ['lnb'], 'c_lnb')
        dma('sp', bsp, bspd, [], ['bsp'], 'c_bsp')
        VIN = Rot('vinb', [bank(0), bank(1)])
        UB = Rot('ub', [bank(2), bank(3)])
        for g in range(NGB):
            ht = HT[:, g]
            hk = [('HT', g)]
            vtok, kvtok = VTOK.next()
            usb, kusb = USB.next()
            for tt in range(4):
                c = g * 4 + tt
                pv, _ = VIN.next()
                bk = ('ps', (VIN.i - 1) % 2)
                for kc in range(8):
                    S.op('pe', lambda e, pv=pv, kc=kc, tt=tt, ht=ht: e.matmul(pv, lhsT=ht[:, kc, tt * 128:(tt + 1) * 128], rhs=wa2[:, kc, 512:1024],
                                                                           start=(kc == 0), stop=(kc == 7)), r=hk + WA2_V, w=[bk])
                gv, kgv = GV.next()
                vn, kvn = VN.next()
                S.op('act', lambda e, gv=gv, pv=pv: e.activation(out=gv, in_=pv, func=AF.Gelu_apprx_tanh), r=[bk], w=[kgv])
                S.op('dve', lambda e, gv=gv, c=c: e.bn_stats(out=lnst[:, c * 6:(c + 1) * 6], in_=gv), r=[kgv], w=[('lnst', c)])
                S.op('dve', lambda e, c=c: e.bn_aggr(out=lnmv[:, c * 2:(c + 1) * 2], in_=lnst[:, c * 6:(c + 1) * 6]), r=[('lnst', c)], w=[('lnmv', c)], t=0.08)
                S.op('dve', lambda e, c=c: e.tensor_scalar(out=lnve[:, c:c + 1], in0=lnmv[:, c * 2 + 1:c * 2 + 2], scalar1=EPS, scalar2=None, op0=ALU.add),
                     r=[('lnmv', c)], w=[('lnve', c)], t=0.08)
                S.op('pool', lambda e, c=c: e.tensor_tensor(out=lnrs[:, c:c + 1], in0=lnve[:, c:c + 1], in1=negh[:, 0:1], op=ALU.pow),
                     r=[('lnve', c), 'negh'], w=[('lnrs', c)], t=0.5)
                S.op('dve', lambda e, gv=gv, vn=vn, c=c: e.tensor_scalar(out=vn, in0=gv, scalar1=lnmv[:, c * 2:c * 2 + 1], scalar2=lnrs[:, c:c + 1],
                                                                       op0=ALU.subtract, op1=ALU.mult), r=[kgv, ('lnmv', c), ('lnrs', c)], w=[kvn])
                S.op('pool', lambda e, vn=vn: e.tensor_tensor(out=vn, in0=vn, in1=lng, op=ALU.mult), r=[kvn, 'lng'], w=[kvn])
                S.op('pool', lambda e, vn=vn, vtok=vtok, tt=tt: e.tensor_tensor(out=vtok[:, tt, :], in0=vn, in1=lnb, op=ALU.add), r=[kvn, 'lnb'], w=[(kvtok, tt)])
            for ct_ in range(4):
                pu, _ = UB.next()
                bk = ('ps', 2 + (UB.i - 1) % 2)
                for kc in range(8):
                    S.op('pe', lambda e, pu=pu, kc=kc, ct_=ct_, ht=ht: e.matmul(pu, lhsT=wa2[:, kc, ct_ * 128:(ct_ + 1) * 128], rhs=ht[:, kc, :],
                                                                             start=(kc == 0), stop=(kc == 7)), r=hk + WA2_U, w=[bk])
                S.op('act', lambda e, pu=pu, ct_=ct_, usb=usb: e.activation(out=usb[:, ct_, :], in_=pu, func=AF.Gelu_apprx_tanh), r=[bk], w=[(kusb, ct_)])
            for pr in range(4):
                for tt in range(4):
                    for half in range(2):
                        grp = 2 * pr + half
                        S.op('pe', lambda e, pr=pr, tt=tt, half=half, grp=grp, vtok=vtok: e.matmul(
                            bank(4 + pr)[half * 64:(half + 1) * 64, tt * 128:(tt + 1) * 128], lhsT=vtok[:, tt, grp * 64:(grp + 1) * 64],
                            rhs=wst[:, grp, :], start=True, stop=True), r=[(kvtok, tt), 'wst'], w=[bkey(4 + pr)], t=0.06)
            for ct_ in range(4):
                pgb, _ = VIN.next()
                bk = ('ps', (VIN.i - 1) % 2)
                for kc in range(8):
                    S.op('pe', lambda e, pgb=pgb, kc=kc, ct_=ct_, ht=ht: e.matmul(pgb, lhsT=wa2[:, kc, 1024 + ct_ * 128:1024 + (ct_ + 1) * 128], rhs=ht[:, kc, :],
                                                                               start=(kc == 0), stop=(kc == 7)), r=hk + WA2_G, w=[bk])
                sgb, ksgb = SGB.next()
                ta, kta = TA.next()
                tb, ktb = TB.next()
                S.op('act', lambda e, sgb=sgb, pgb=pgb: e.activation(out=sgb, in_=pgb, func=AF.Silu), r=[bk], w=[ksgb])
                S.op('dve', lambda e, ta=ta, ct_=ct_: e.tensor_tensor(out=ta, in0=bank(4 + ct_), in1=bsp[:, ct_, :], op=ALU.add), r=[bkey(4 + ct_), 'bsp'], w=[kta])
                S.op('pool', lambda e, tb=tb, sgb=sgb, ct_=ct_, usb=usb: e.tensor_tensor(out=tb, in0=usb[:, ct_, :], in1=sgb, op=ALU.mult), r=[(kusb, ct_), ksgb], w=[ktb])
                S.op('dve', lambda e, ta=ta, tb=tb, ct_=ct_, g=g: e.tensor_tensor(out=GM[:, ct_, g * G:(g + 1) * G], in0=ta, in1=tb, op=ALU.mult), r=[kta, ktb], w=[('GM', ct_, g)])
        if debug:
            S.barrier()
            dma('sp', dbg["d_gm"], arena[:, OFF_GM // 2:(OFF_GM + 16 * KB) // 2], [], ['dbg5'], 'dbg5')
        S.barrier()

        S.reorder = False
        WC = Bump(arena, OFF_FLEX, ARENA)
        wga = WC.alloc(8 * KB, BF16).rearrange("p (k n) -> p k n", k=8)
        wpa = WC.alloc(8 * KB, BF16).rearrange("p (k n) -> p k n", k=4)
        wg = WC.alloc(32 * KB, BF16).rearrange("p (k n) -> p k n", k=8)
        wout = WC.alloc(16 * KB, BF16).rearrange("p (k n) -> p k n", k=8)
        WC_WORK = WC.off
        PT = Rot('pt', [WC.alloc(2 * KB, BF16) for _ in range(4)])
        OSB = [WC.alloc(2 * KB, F32) for _ in range(2)]
        _rs = WC.alloc(2 * KB, F32)
        _rs2 = WC.alloc(2 * KB, F32)
        RS = [_rs, _rs]
        RS2 = [_rs2, _rs2]
        for kc in range(0, 8, 2):
            dma('pool', wg[:, kc:kc + 2, :], w_g[:, kc:kc + 2, :], [], [('wg', kc)], ('wg', kc))
            fold_gain(wg, [kc, kc + 1], [('wg', kc)])
        WG_KEYS = [('wg', kc) for kc in range(0, 8, 2)]
        for kc in range(0, 8, 4):
            dma('pool', wout[:, kc:kc + 4, :], w_o[:, kc:kc + 4, :], [], [('wout', kc)], ('wout', kc))
        WOUT_KEYS = [('wout', 0), ('wout', 4)]
        dma('pool', wga, w_ga, [], ['wga'], 'c_wga')
        fold_gain(wga, range(8), ['wga'])
        dma('pool', wpa, w_pa, [], ['wpa'], 'c_wpa')

        SC = 0.125
        OBK = 6
        passes = [(qg, j) for qg in range(NG_OWN if max_phase >= 3 else 0) for j in range(4)]
        NT = len(passes) * 64

        def qkmm(t):
            qg, j = passes[t // 64]
            kt = t % 64
            sb_ = t % 3
            qA = QT[0:64, j, qg * G:(qg + 1) * G]
            qB = QT[64:128, j, qg * G:(qg + 1) * G]
            rk = [('KT', kt // 4), ('QT', j, qg)]
            S.op('pe', lambda e: e.matmul(PS[sb_][:, 0:512], lhsT=KT[0:64, kt * 128:(kt + 1) * 128], rhs=qA, start=True, stop=True),
                 r=rk, w=[bkey(2 * sb_)])
            S.op('pe', lambda e: e.matmul(PS[sb_][:, 512:1024], lhsT=KT[64:128, kt * 128:(kt + 1) * 128], rhs=qB, start=True, stop=True),
                 r=rk, w=[bkey(2 * sb_ + 1)])

        def expo(t):
            sb_ = t % 3
            pt, kpt = PT.next()
            S.op('act', lambda e: e.activation(out=pt, in_=PS[sb_][:, :], func=AF.Exp, scale=SC, bias=-12.0),
                 r=[bkey(2 * sb_), bkey(2 * sb_ + 1)], w=[kpt], t=1.0)
            return pt, kpt

        def normalise(qg, j, last=False):
            S.op('dve', lambda e: e.tensor_copy(out=OSB[0], in_=bank(OBK)), r=[bkey(OBK)], w=['osb0'])
            S.op('dve', lambda e: e.tensor_copy(out=OSB[1], in_=bank(OBK + 1)), r=[bkey(OBK + 1)], w=['osb1'])
            S.op('dve', lambda e: e.reciprocal(out=RS[0][64:128, :], in_=OSB[0][64:128, :]), r=['osb0'], w=['rs0'])
            if last:
                S.op('act', lambda e: e.activation(out=RS[1][0:64, :], in_=OSB[1][0:64, :], func=AF.Ln), r=['osb1'], w=['rs1'])
                S.op('act', lambda e: e.activation(out=RS[1][0:64, :], in_=RS[1][0:64, :], func=AF.Exp, scale=-1.0), r=['rs1'], w=['rs1'])
            else:
                S.op('dve', lambda e: e.reciprocal(out=RS[1][0:64, :], in_=OSB[1][0:64, :]), r=['osb1'], w=['rs1'])
            S.op('dve', lambda e: e.tensor_copy(out=RS2[0][0:64, :], in_=RS[0][64:128, :]), r=['rs0'], w=['rs20'])
            S.op('dve', lambda e: e.tensor_copy(out=RS2[1][64:128, :], in_=RS[1][0:64, :]), r=['rs1'], w=['rs21'])
            S.op('dve', lambda e: e.tensor_tensor(out=ATT[0:64, j, qg * G:(qg + 1) * G], in0=OSB[0][0:64, :], in1=RS2[0][0:64, :], op=ALU.mult),
                 r=['osb0', 'rs20'], w=[('ATT0', j, qg)])
            S.op('dve', lambda e: e.tensor_tensor(out=ATT[64:128, j, qg * G:(qg + 1) * G], in0=OSB[1][64:128, :], in1=RS2[1][64:128, :], op=ALU.mult),
                 r=['osb1', 'rs21'], w=[('ATT1', j, qg)])

        def pvmm(t, pt, kpt):
            qg, j = passes[t // 64]
            kt = t % 64
            vk = [('V0', kt // 4), ('V1', kt // 4), 'vones']
            S.op('pe', lambda e: e.matmul(bank(OBK), lhsT=Vsb[:, kt, 0:128], rhs=pt[:, 0:512], start=(kt == 0), stop=(kt == 63)),
                 r=[kpt] + vk, w=[bkey(OBK)])
            S.op('pe', lambda e: e.matmul(bank(OBK + 1), lhsT=Vsb[:, kt, 64:192], rhs=pt[:, 512:1024], start=(kt == 0), stop=(kt == 63)),
                 r=[kpt] + vk, w=[bkey(OBK + 1)])
            if kt == 63:
                normalise(qg, j, last=(t == NT - 1))

        if NT:
            qkmm(0)
            qkmm(1)
        for t in range(0, NT, 2):
            p0 = expo(t)
            p1 = expo(t + 1)
            if t + 2 < NT:
                qkmm(t + 2)
                qkmm(t + 3)
            pvmm(t, *p0)
            pvmm(t + 1, *p1)
        hoisted = {}
        if NT and max_phase >= 4:
            for j in range(2):
                for kc in range(8):
                    S.op('pe', lambda e, kc=kc, j=j: e.matmul(bank(j), lhsT=wga[:, kc, j * 128:(j + 1) * 128], rhs=HT[:, 0][:, kc, :],
                                                           start=(kc == 0), stop=(kc == 7)), r=[('HT', 0), 'wga'], w=[bkey(j)])
                hoisted[(0, j)] = True
        if debug:
            S.barrier()
            dma('sp', dbg["d_attn"], arena[:, OFF_ATTN // 2:(OFF_ATTN + 16 * KB) // 2], [], ['dbg6'], 'dbg6')
        S.barrier()

        S.reorder = True
        WD = Bump(arena, OFF_KT, OFF_HT)
        wpg = WD.alloc(8 * KB, BF16).rearrange("p (k n) -> p k n", k=4)
        fgt = WD.alloc(4 * KB, F32)
        WD2 = Bump(arena, WC_WORK, ARENA)
        XC = Rot('xc', [WD2.alloc(4 * KB, F32) for _ in range(2)] + [WD.alloc(4 * KB, F32)])
        SGA = Rot('sga', [WD.alloc(2 * KB, F32) for _ in range(2)])
        AT = Rot('aT', [WD.alloc(4 * KB, BF16).rearrange("p (j t) -> p j t", j=4) for _ in range(2)])
        G0 = Rot('g0', [WD.alloc(2 * KB, F32) for _ in range(2)])
        G1 = Rot('g1', [WD.alloc(2 * KB, F32) for _ in range(2)])
        TC = Rot('tc', [WD.alloc(2 * KB, F32) for _ in range(2)])
        TD = Rot('td', [WD.alloc(2 * KB, F32) for _ in range(2)])
        YT = Rot('yT', [WD.alloc(8 * KB, BF16).rearrange("p (k t) -> p k t", k=8) for _ in range(1)])
        RR = Rot('rr', [WD2.alloc(4 * KB, F32) for _ in range(2)] + [WD.alloc(4 * KB, F32)])
        dma('pool', wpg, w_pg, [], ['wpg'], 'c_wpg')
        dma('sp', fgt, fgv.partition_broadcast(P), [], ['fgt'], 'c_fgt')
        GAB = Rot('gab', [bank(0), bank(1)])
        tails = []

        def emit_tail(rr, krr, c):
            S.op('dve', lambda e: e.scalar_tensor_tensor(out=rr, in0=rr, scalar=frs[:, c:c + 1], in1=fgt, op0=ALU.mult, op1=ALU.mult),
                 r=[(krr, 0), (krr, 1), ('frs', c), 'fgt'], w=[(krr, 0), (krr, 1)], t=1.3)
            dma('sp', out[c * P:(c + 1) * P, :], rr, [(krr, 0), (krr, 1)], [('out', c)], ('o', c % 3))
        OB = Rot('ob', [bank(6), bank(7)])
        for g in range(NG_OWN if max_phase >= 4 else 0):
            ht = HT[:, g]
            hk = [('HT', g)]
            aT, kaT = AT.next()
            yT, kyT = YT.next()
            for j in range(4):
                pga, _ = GAB.next()
                bk = ('ps', (GAB.i - 1) % 2)
                for kc in range(8):
                    if (g, j) in hoisted:
                        continue
                    S.op('pe', lambda e, pga=pga, kc=kc, j=j, ht=ht: e.matmul(pga, lhsT=wga[:, kc, j * 128:(j + 1) * 128], rhs=ht[:, kc, :],
                                                                           start=(kc == 0), stop=(kc == 7)), r=hk + ['wga'], w=[bk])
                sga, ksga = SGA.next()
                S.op('act', lambda e, sga=sga, pga=pga: e.activation(out=sga, in_=pga, func=AF.Silu), r=[bk], w=[ksga])
                S.op('dve', lambda e, sga=sga, j=j, g=g, aT=aT: e.tensor_tensor(out=aT[:, j, :], in0=sga, in1=ATT[:, j, g * G:(g + 1) * G], op=ALU.mult),
                     r=[ksga, ('ATT0', j, g), ('ATT1', j, g)], w=[(kaT, j)])
            for ot in range(8):
                g0, kg0 = G0.next()
                g1, kg1 = G1.next()
                tc, ktc = TC.next()
                td, ktd = TD.next()
                for kc in range(8):
                    S.op('pe', lambda e, ot=ot, kc=kc, ht=ht: e.matmul(bank(4), lhsT=wg[:, kc, ot * 128:(ot + 1) * 128], rhs=ht[:, kc, :], start=(kc == 0), stop=(kc == 7)),
                         r=hk + WG_KEYS, w=[bkey(4)])
                S.op('act', lambda e, g0=g0, ot=ot: e.activation(out=g0, in_=bank(4), func=AF.Sigmoid, bias=bm_t[:, ot:ot + 1]), r=[bkey(4), 'bm'], w=[kg0])
                for kc in range(8):
                    S.op('pe', lambda e, ot=ot, kc=kc, ht=ht: e.matmul(bank(5), lhsT=wg[:, kc, 1024 + ot * 128:1024 + (ot + 1) * 128], rhs=ht[:, kc, :],
                                                                    start=(kc == 0), stop=(kc == 7)), r=hk + WG_KEYS, w=[bkey(5)])
                S.op('act', lambda e, g1=g1, ot=ot: e.activation(out=g1, in_=bank(5), func=AF.Sigmoid, bias=bm_t[:, 8 + ot:9 + ot]), r=[bkey(5), 'bm'], w=[kg1])
                for j in range(4):
                    S.op('pe', lambda e, ot=ot, j=j, aT=aT: e.matmul(bank(2), lhsT=wpa[:, j, ot * 128:(ot + 1) * 128], rhs=aT[:, j, :], start=(j == 0), stop=(j == 3)),
                         r=['wpa', (kaT, j)], w=[bkey(2)])
                S.op('dve', lambda e, tc=tc, g0=g0: e.tensor_tensor(out=tc, in0=bank(2), in1=g0, op=ALU.mult), r=[bkey(2), kg0], w=[ktc])
                for c4 in range(4):
                    S.op('pe', lambda e, ot=ot, c4=c4, g=g: e.matmul(bank(3), lhsT=wpg[:, c4, ot * 128:(ot + 1) * 128], rhs=GM[:, c4, g * G:(g + 1) * G],
                                                                  start=(c4 == 0), stop=(c4 == 3)), r=['wpg', ('GM', c4, g)], w=[bkey(3)])
                S.op('dve', lambda e, td=td, g1=g1: e.tensor_tensor(out=td, in0=bank(3), in1=g1, op=ALU.mult), r=[bkey(3), kg1], w=[ktd])
                S.op('pool', lambda e, tc=tc, td=td, ot=ot, yT=yT: e.tensor_tensor(out=yT[:, ot, :], in0=tc, in1=td, op=ALU.add), r=[ktc, ktd], w=[(kyT, ot)])
            ykeys = [(kyT, ot) for ot in range(8)]
            for tt in range(4):
                c = g * 4 + tt
                xc, kxc = XC.next()
                rr, krr = RR.next()
                dma('sp', xc, x[c * P:(c + 1) * P, :], [], [kxc], kxc)
                for half in range(2):
                    po, _ = OB.next()
                    bk = ('ps', 6 + (OB.i - 1) % 2)
                    for ot in range(8):
                        S.op('pe', lambda e, po=po, ot=ot, tt=tt, half=half, yT=yT: e.matmul(po, lhsT=yT[:, ot, tt * 128:(tt + 1) * 128],
                                                                                          rhs=wout[:, ot, half * 512:(half + 1) * 512], start=(ot == 0), stop=(ot == 7)),
                             r=ykeys + WOUT_KEYS, w=[bk])
                    S.op('dve', lambda e, po=po, half=half, rr=rr, xc=xc: e.tensor_tensor(out=rr[:, half * 512:(half + 1) * 512], in0=po,
                                                                                       in1=xc[:, half * 512:(half + 1) * 512], op=ALU.add),
                         r=[bk, kxc], w=[(krr, half)])
                S.op('act', lambda e, rr=rr, c=c, xc=xc: e.activation(out=xc, in_=rr, func=AF.Square, accum_out=fss[:, c:c + 1]),
                     r=[(krr, 0), (krr, 1)], w=[kxc, ('fss', c)], t=1.0)
                S.op('dve', lambda e, c=c: e.tensor_scalar(out=fms[:, c:c + 1], in0=fss[:, c:c + 1], scalar1=1.0 / D, scalar2=EPS, op0=ALU.mult, op1=ALU.add),
                     r=[('fss', c)], w=[('fms', c)], t=0.08)
                S.op('pool', lambda e, c=c: e.tensor_tensor(out=frs[:, c:c + 1], in0=fms[:, c:c + 1], in1=negh[:, 0:1], op=ALU.pow),
                     r=[('fms', c), 'negh'], w=[('frs', c)], t=0.5)
                tails.append((rr, krr, c))
                if len(tails) >= 2:
                    emit_tail(*tails.pop(0))
        while tails:
            emit_tail(*tails.pop(0))
        S.barrier()
        S.emit(nc, st)
    return nc, S


def _rope_tables(r):
    pos = np.arange(S_OWN) + r * S_OWN
    row = (pos // 64).astype(np.float64)
    col = (pos % 64).astype(np.float64)
    inv = 10000.0 ** (-(np.arange(0, 32, 2, dtype=np.float64) / 32.0))
    ct = np.zeros((P, S_OWN), np.float32)
    sn = np.zeros((P, S_OWN), np.float32)
    for p in range(P):
        d = p % 64
        idx = row if d < 32 else col
        dd = d % 32
        f = dd % 16
        ang = idx * inv[f]
        ct[p] = np.cos(ang)
        sn[p] = -np.sin(ang) if dd < 16 else np.sin(ang)
    return ct, sn


def _tile_rows(w, nk):
    return np.ascontiguousarray(w.reshape(nk, P, w.shape[1]).transpose(1, 0, 2))


def prepare_inputs(x, norm_gain, w_in, q_gain, k_gain, w_proj_attn, ln_v_gain, ln_v_bias,
                   w_spatial, b_spatial, w_proj_gmlp, b_merge, w_out, final_gain):
    f = lambda a: np.ascontiguousarray(np.asarray(a, dtype=np.float32))
    x = f(x); w_in = f(w_in)[0]
    wq, wk, wv, wga, wu, wvi, wgb, wgm = np.split(w_in, np.cumsum([512, 128, 128, 512, 512, 512, 512])[:], axis=1)
    pair_cols = np.concatenate([np.r_[j * 64:(j + 1) * 64, (4 + j) * 64:(5 + j) * 64] for j in range(4)])
    shared = {
        "w_kvq": _tile_rows(np.concatenate([wk, wv, wq[:, pair_cols]], axis=1), 8),
        "w_a2": _tile_rows(np.concatenate([wu, wvi, wgb], axis=1), 8),
        "w_ga": _tile_rows(wga[:, pair_cols], 8),
        "w_g": _tile_rows(wgm, 8),
        "w_pa": _tile_rows(f(w_proj_attn)[0][pair_cols, :], 4),
        "w_pg": _tile_rows(f(w_proj_gmlp)[0], 4),
        "w_o": _tile_rows(f(w_out)[0], 8),
        "wsT": np.ascontiguousarray(f(w_spatial)[0].transpose(2, 0, 1)),
        "ng": f(norm_gain)[0],
        "ng_t": np.ascontiguousarray(f(norm_gain)[0].reshape(8, P).T),
        "gqk": np.ascontiguousarray(np.stack([np.tile(f(q_gain)[0], 2), np.tile(f(k_gain)[0], 2)], axis=1)),
        "gq_row": f(q_gain)[0],
        "gk_row": f(k_gain)[0],
        "lng": f(ln_v_gain)[0],
        "lnb": f(ln_v_bias)[0],
        "bm": np.ascontiguousarray(f(b_merge)[0].reshape(16, P).T),
        "fg": f(final_gain),
        "ident": np.eye(P, dtype=np.float32),
    }
    bs = f(b_spatial)[0]
    bsp = np.zeros((P, 4, 4, 128), np.float32)
    for pr in range(4):
        bsp[0:64, pr, :, :] = bs[2 * pr][None, None, :]
        bsp[64:128, pr, :, :] = bs[2 * pr + 1][None, None, :]
    shared["bsp"] = np.ascontiguousarray(bsp.reshape(P, 4, 512))
    partner = np.array([(p // 32) * 32 + ((p % 32) + 16) % 32 for p in range(P)])
    swm = np.zeros((P, P), np.float32)
    swm[partner, np.arange(P)] = 1.0
    shared["swm"] = swm
    bo = np.zeros((P, P), np.float32)
    bo[0:64, 0:64] = 1.0
    bo[64:128, 64:128] = 1.0
    shared["bo"] = bo
    tabs = [_rope_tables(r) for r in range(4)]
    in_maps = []
    for c in range(8):
        b, r = c // 4, c % 4
        m = dict(shared)
        m["x"] = np.ascontiguousarray(x[b, r * S_OWN:(r + 1) * S_OWN])
        m["ctab"], m["stab"] = tabs[r]
        in_maps.append(m)
    return in_maps


_CACHE = {}


def kernel(**inputs):
    in_maps = prepare_inputs(**inputs)
    if "nc" not in _CACHE:
        _CACHE["nc"] = build_program(False)[0]
    nc = _CACHE["nc"]
    res = run_bass_kernel_spmd(nc, in_maps, core_ids=list(range(8)))
    outp = np.zeros((2, S_ALL, D), np.float32)
    for c in range(8):
        b, r = c // 4, c % 4
        outp[b, r * S_OWN:(r + 1) * S_OWN, :] = np.asarray(res.results[c]["out"], dtype=np.float32)
    return outp
```

```python
import contextlib
import numpy as np
import concourse.bass as bass
import concourse.mybir as mybir
from concourse.bass_utils import run_bass_kernel_spmd

F32 = mybir.dt.float32
BF16 = mybir.dt.bfloat16
AF = mybir.ActivationFunctionType
ALU = mybir.AluOpType

P = 128
D = 1024
S_ALL = 8192
S_OWN = 2048
NG_ALL = 16
NG_OWN = 4
G = 512
EPS = 1e-6
ENGS = ('pe', 'act', 'dve', 'pool', 'sp')


class Sched:
    DEF_T = dict(pe=0.25, act=0.7, dve=0.7, pool=1.3, sp=0.05)

    def __init__(self):
        self.ops = []
        self.lastw = {}
        self.readers = {}
        self.marks = []
        self.reorder = True

    def op(self, eng, fn, r=(), w=(), dma=None, t=None, inc=16):
        i = len(self.ops)
        raw, other = set(), set()
        for b in r:
            lw = self.lastw.get(b)
            if lw is not None:
                raw.add(lw)
        for b in w:
            lw = self.lastw.get(b)
            if lw is not None:
                other.add(lw)
            other.update(self.readers.get(b, ()))
        for b in r:
            self.readers.setdefault(b, []).append(i)
        for b in w:
            self.lastw[b] = i
            self.readers[b] = []
        raw.discard(i)
        other.discard(i)
        if t is None:
            t = self.DEF_T[eng]
        self.ops.append(dict(eng=eng, fn=fn, raw=raw, other=other - raw, dma=dma, t=t, inc=inc))
        return i

    def barrier(self):
        self.marks.append((len(self.ops), self.reorder))

    def _schedule(self, idxs):
        ops = self.ops
        iset = set(idxs)
        preds = {i: set(d for d in (ops[i]['raw'] | ops[i]['other']) if d in iset) for i in idxs}
        last_stream = {}
        for i in idxs:
            k = ops[i]['dma']
            if k is not None:
                if k in last_stream:
                    preds[i].add(last_stream[k])
                last_stream[k] = i
        succs = {i: [] for i in idxs}
        indeg = {}
        for i in idxs:
            indeg[i] = len(preds[i])
            for d in preds[i]:
                succs[d].append(i)
        HOP = 0.15
        rtime = {i: 0.0 for i in idxs}
        ready = {e: [] for e in ENGS}
        for i in idxs:
            if indeg[i] == 0:
                ready[ops[i]['eng']].append(i)
        free = {e: 0.0 for e in ENGS}
        order = []
        n = len(idxs)
        while len(order) < n:
            best = None
            for e in ENGS:
                for i in ready[e]:
                    key = (max(rtime[i], free[e]), i)
                    if best is None or key < best[0]:
                        best = (key, i, e)
            (st_, _), i, e = best
            ready[e].remove(i)
            o = ops[i]
            if o['dma'] is not None:
                free[e] = st_ + (0.05 if e == 'sp' else 0.6)
                fin = st_ + 2.0 + o['t']
            else:
                free[e] = st_ + o['t']
                fin = free[e]
            order.append(i)
            for j in succs[i]:
                rtime[j] = max(rtime[j], fin + HOP)
                indeg[j] -= 1
                if indeg[j] == 0:
                    ready[ops[j]['eng']].append(j)
        return order

    def emit(self, nc, stack):
        ops = self.ops
        seq = []
        start = 0
        for (m, reorder) in self.marks:
            idxs = list(range(start, m))
            seq += [('op', i) for i in (self._schedule(idxs) if reorder else idxs)]
            seq.append(('bar',))
            start = m
        assert start == len(ops), "program must end with a barrier"
        for i, o in enumerate(ops):
            deps = set()
            for d in (o['raw'] | o['other']):
                p = ops[d]
                if p['dma'] is None and o['dma'] is None and p['eng'] == o['eng'] and o['eng'] == 'pe':
                    continue
                deps.add(d)
            o['deps'] = deps
        has_dep = [False] * len(ops)
        for o in ops:
            for d in o['deps']:
                has_dep[d] = True
        last = {}
        final = []
        for ent in seq:
            if ent[0] == 'op':
                i = ent[1]
                o = ops[i]
                k = ('dma', o['dma']) if o['dma'] is not None else ('eng', o['eng'])
                last[k] = i
                final.append(ent)
            else:
                dl = set(last.values())
                for d in dl:
                    has_dep[d] = True
                for e in ENGS:
                    final.append(('wait', e, dl))
        eng_sem, eng_cnt, dma_sem, dma_cnt = {}, {}, {}, {}
        for ent in final:
            if ent[0] != 'op':
                continue
            i = ent[1]
            o = ops[i]
            o['sig'] = None
            if o['dma'] is not None:
                k = o['dma']
                if k not in dma_sem:
                    dma_sem[k] = stack.enter_context(nc.semaphore("d%d" % len(dma_sem)))
                    dma_cnt[k] = 0
                dma_cnt[k] += o['inc']
                o['sig'] = (dma_sem[k], dma_cnt[k])
            else:
                e = o['eng']
                if e not in eng_sem:
                    eng_sem[e] = stack.enter_context(nc.semaphore("e_" + e))
                    eng_cnt[e] = 0
                if has_dep[i]:
                    eng_cnt[e] += 1
                    o['sig'] = (eng_sem[e], eng_cnt[e])
        self.stats = dict(n_ops=len(ops), sems=len(eng_sem) + len(dma_sem), eng_cnt=dict(eng_cnt))
        block = stack.enter_context(nc.Block())

        def run(ename, eng):
            known = {}

            def waits(depset):
                need = {}
                for d in depset:
                    sig = ops[d]['sig']
                    assert sig is not None
                    s, v = sig
                    if id(s) not in need or need[id(s)][1] < v:
                        need[id(s)] = (s, v)
                for key, (s, v) in need.items():
                    if known.get(key, 0) >= v:
                        continue
                    eng.wait_ge(s, v)
                    known[key] = v

            for ent in final:
                if ent[0] == 'wait':
                    if ent[1] == ename:
                        waits(ent[2])
                    continue
                o = ops[ent[1]]
                if o['eng'] != ename:
                    continue
                waits(o['deps'])
                ins = o['fn'](eng)
                if o['sig'] is not None:
                    ins.then_inc(o['sig'][0], o['inc'] if o['dma'] is not None else 1)

        @block.tensor
        def _(e):
            run('pe', e)

        @block.scalar
        def _(e):
            run('act', e)

        @block.vector
        def _(e):
            run('dve', e)

        @block.gpsimd
        def _(e):
            run('pool', e)

        @block.sync
        def _(e):
            run('sp', e)


class Rot:
    def __init__(self, name, aps):
        self.name, self.aps, self.i = name, aps, 0

    def next(self):
        k = self.i % len(self.aps)
        self.i += 1
        return self.aps[k], (self.name, k)


class Bump:
    def __init__(self, arena, start, end):
        self.arena, self.off, self.end = arena, start, end

    def alloc(self, nbytes, dt, shape=None):
        nbytes = (nbytes + 63) // 64 * 64
        assert self.off + nbytes <= self.end, (self.off, nbytes, self.end)
        ap = self.arena[:, self.off // 2:(self.off + nbytes) // 2]
        self.off += nbytes
        if dt != BF16:
            ap = ap.bitcast(dt)
        return ap


KB = 1024
OFF_KT, OFF_V, OFF_QT, OFF_HT, OFF_GM, OFF_ATTN, OFF_FLEX, ARENA = (
    0, 16 * KB, 40 * KB, 56 * KB, 88 * KB, 104 * KB, 120 * KB, 200 * KB)


def build_program(debug=False, max_phase=4, n_a_groups=NG_OWN):
    nc = bass.Bass("TRN2", target_bir_lowering=False)

    def din(name, shape, dt=F32):
        return nc.dram_tensor(name, shape, dt, kind="ExternalInput").ap()

    x = din("x", [S_OWN, D])
    ctab = din("ctab", [P, S_OWN])
    stab = din("stab", [P, S_OWN])
    cc_srcK = nc.dram_tensor("cc_srcK", [P, S_OWN], BF16).ap()
    cc_dstK = nc.dram_tensor("cc_dstK", [4 * P, S_OWN], BF16).ap()
    cc_srcV = nc.dram_tensor("cc_srcV", [P, 16 * 192], BF16).ap()
    cc_dstV = nc.dram_tensor("cc_dstV", [4 * P, 16 * 192], BF16).ap()
    w_kvq = din("w_kvq", [P, 8, 768])
    w_a2 = din("w_a2", [P, 8, 1536])
    w_ga = din("w_ga", [P, 8, 512])
    w_g = din("w_g", [P, 8, 2048])
    w_pa = din("w_pa", [P, 4, 1024])
    w_pg = din("w_pg", [P, 4, 1024])
    w_o = din("w_o", [P, 8, 1024])
    wsT = din("wsT", [P, 8, 128])
    ngv = din("ng", [D])
    ngtd = din("ng_t", [P, 8])
    gqk = din("gqk", [P, 2])
    gq_row = din("gq_row", [64])
    gk_row = din("gk_row", [64])
    lngv = din("lng", [512])
    lnbv = din("lnb", [512])
    bspd = din("bsp", [P, 4, 512])
    bmd = din("bm", [P, 16])
    fgv = din("fg", [D])
    identd = din("ident", [P, P])
    swmd = din("swm", [P, P])
    bod = din("bo", [P, P])
    out = nc.dram_tensor("out", [S_OWN, D], F32, kind="ExternalOutput").ap()
    dbg = {}
    if debug:
        for nm, shp, dt in [("d_kt", [P, S_ALL], BF16), ("d_v", [P, 64 * 192], BF16), ("d_qt", [P, 4 * S_OWN], BF16),
                            ("d_ht", [P, 4 * 8 * 512], BF16), ("d_gm", [P, 4 * S_OWN], BF16),
                            ("d_attn", [P, 4 * S_OWN], BF16)]:
            dbg[nm] = nc.dram_tensor(nm, shp, dt, kind="ExternalOutput").ap()

    S = Sched()
    with contextlib.ExitStack() as st:
        arena = st.enter_context(nc.sbuf_tensor("arena", [P, ARENA // 2], BF16))

        def sb(name, shape, dt=F32):
            return st.enter_context(nc.sbuf_tensor("s_" + name, shape, dt))

        PS = [st.enter_context(nc.psum_tensor("ps%d" % i, [P, 1024], F32)) for i in range(4)]

        def bank(b):
            return PS[b // 2][:, (b % 2) * 512:(b % 2) * 512 + 512]

        def bkey(b):
            return ('ps', b)

        ident = sb("ident", [P, P], BF16)
        swm = sb("swm", [P, P], BF16)
        bo = sb("bo", [P, P], BF16)
        gqk_t = sb("gqk", [P, 2], F32)
        bm_t = sb("bm", [P, 16], F32)
        negh = sb("negh", [P, 1], F32)
        ngT = sb("ngT", [P, 8], F32)
        gq_b = sb("gq_b", [P, 64], F32)
        gk_b = sb("gk_b", [P, 64], F32)
        mqk = sb("mqk", [P, 2], F32)
        nbias = sb("nbias", [P, 1], F32)
        ssall = sb("ssall", [P, 64], F32)
        msall = sb("msall", [P, 64], F32)
        rsall = sb("rsall", [P, 64], F32)
        fss = sb("fss", [P, 16], F32)
        fms = sb("fms", [P, 16], F32)
        frs = sb("frs", [P, 16], F32)
        lnst = sb("lnst", [P, 16 * 6], F32)
        lnmv = sb("lnmv", [P, 16 * 2], F32)
        lnve = sb("lnve", [P, 16], F32)
        lnrs = sb("lnrs", [P, 16], F32)

        KT = arena[:, OFF_KT // 2:(OFF_KT + 16 * KB) // 2]
        Vsb = arena[:, OFF_V // 2:(OFF_V + 24 * KB) // 2].rearrange("p (k c) -> p k c", c=192)
        QT = arena[:, OFF_QT // 2:(OFF_QT + 16 * KB) // 2].rearrange("p (j t) -> p j t", j=4)
        HT = arena[:, OFF_HT // 2:(OFF_HT + 32 * KB) // 2].rearrange("p (g k t) -> p g k t", g=4, k=8)
        GM = arena[:, OFF_GM // 2:(OFF_GM + 16 * KB) // 2].rearrange("p (c t) -> p c t", c=4)
        ATT = arena[:, OFF_ATTN // 2:(OFF_ATTN + 16 * KB) // 2].rearrange("p (j t) -> p j t", j=4)

        def dma(eng, out_ap, in_ap, r, w, stream):
            S.op(eng, lambda e: e.dma_start(out=out_ap, in_=in_ap), r=r, w=w, dma=stream)

        dma('pool', ident[:], identd, [], ['ident'], 'c_ident')
        dma('pool', swm[:], swmd, [], ['swm'], 'c_swm')
        dma('pool', bo[:], bod, [], ['bo'], 'c_bo')
        dma('sp', gqk_t[:], gqk, [], ['gqk'], 'c_gqk')
        dma('sp', bm_t[:], bmd, [], ['bm'], 'c_bm')
        dma('sp', ngT[:], ngtd, [], ['ngT'], 'c_ngT')

        def fold_gain(w_ap, kcs, keys, eng='dve'):
            for kc in kcs:
                S.op(eng, lambda e, kc=kc: e.tensor_scalar(out=w_ap[:, kc, :], in0=w_ap[:, kc, :], scalar1=ngT[:, kc:kc + 1], scalar2=None, op0=ALU.mult),
                     r=list(keys) + ['ngT'], w=list(keys), t=0.3)
        S.op('pool', lambda e: e.memset(negh[:], -0.5), w=['negh'])
        dma('sp', gq_b[:], gq_row.partition_broadcast(P), [], ['gq_b'], 'c_gqb')
        dma('sp', gk_b[:], gk_row.partition_broadcast(P), [], ['gk_b'], 'c_gkb')
        S.op('dve', lambda e: e.reduce_max(out=mqk[:, 0:1], in_=gq_b[:], axis=mybir.AxisListType.X, apply_absolute_value=True), r=['gq_b'], w=['mq'], t=0.1)
        S.op('dve', lambda e: e.reduce_max(out=mqk[:, 1:2], in_=gk_b[:], axis=mybir.AxisListType.X, apply_absolute_value=True), r=['gk_b'], w=['mk'], t=0.1)
        S.op('dve', lambda e: e.tensor_scalar(out=nbias[:], in0=mqk[:, 0:1], scalar1=mqk[:, 1:2], scalar2=-8.0, op0=ALU.mult, op1=ALU.mult),
             r=['mq', 'mk'], w=['nbias'], t=0.1)
        S.op('pool', lambda e: e.memset(Vsb[:, 0:16, 64:128], 1.0), w=['vones'], t=1.5)

        WA_END = OFF_FLEX + 56 * KB
        WA = Bump(arena, OFF_GM, WA_END)
        wa2 = arena[:, WA_END // 2:ARENA // 2].rearrange("p (k n) -> p k n", k=8)
        wkvq = WA.alloc(12 * KB, BF16).rearrange("p (k n) -> p k n", k=8)
        XT = Rot('xt', [WA.alloc(4 * KB, F32) for _ in range(4)])
        HB = Rot('hb', [WA.alloc(8 * KB, BF16).rearrange("p (t d) -> p t d", t=4) for _ in range(3)])
        CT = Rot('ct', [WA.alloc(2 * KB, F32) for _ in range(3)])
        STb = Rot('st', [WA.alloc(2 * KB, F32) for _ in range(3)])
        HTG = None
        SQ = Rot('sq', [WA.alloc(1 * KB, BF16) for _ in range(2)])
        KGB = Rot('kgb', [WA.alloc(1 * KB, BF16) for _ in range(2)])
        T1 = Rot('t1', [WA.alloc(2 * KB, F32) for _ in range(3)])
        T2 = Rot('t2', [WA.alloc(2 * KB, F32) for _ in range(2)])
        SR = Rot('sr', [WA.alloc(2 * KB, F32) for _ in range(2)])
        RI = Rot('ri', [WA.alloc(2 * KB, F32) for _ in range(2)])

        WKV_KEYS = [('wkvq', 'kv', 0), ('wkvq', 'kv', 1)]
        WQ_KEYS = [('wkvq', 'q', 0), ('wkvq', 'q', 1)]

        def issue_wkvq():
            xkeys = [('xt', k) for k in range(4)]
            for (nm, c0, c1) in (('kv', 0, 256), ('q', 256, 768)):
                for kh in range(2):
                    dma('pool', wkvq[:, kh * 4:(kh + 1) * 4, c0:c1], w_kvq[:, kh * 4:(kh + 1) * 4, c0:c1], xkeys, [('wkvq', nm, kh)], ('wkvq', nm, kh))
                    fold_gain(wkvq[:, :, c0:c1], range(kh * 4, (kh + 1) * 4), [('wkvq', nm, kh)])

        PROJ_B = [2, 3, 4]
        proj_i = [0]
        evac_flip = [0]
        rope_q = []

        def rope_r1a(ps_ap, ps_key, gain_col, dst_ap, dst_key, ct_ap, ct_key, st_ap, st_key):
            c = dict(dst_ap=dst_ap, dst_key=dst_key, st_ap=st_ap, st_key=st_key)
            c['sq'], c['ksq'] = SQ.next()
            c['kgb'], c['kkgb'] = KGB.next()
            c['t1'], c['kt1'] = T1.next()
            sq, kgb, t1 = c['sq'], c['kgb'], c['t1']
            S.op('act', lambda e: e.activation(out=sq, in_=ps_ap, func=AF.Square), r=[ps_key], w=[c['ksq'], ps_key])
            S.op('act', lambda e: e.activation(out=kgb, in_=ps_ap, func=AF.Copy, scale=gain_col), r=[ps_key, 'gqk'], w=[c['kkgb'], ps_key])
            S.op('dve', lambda e: e.scalar_tensor_tensor(out=t1, in0=ps_ap, scalar=gain_col, in1=ct_ap, op0=ALU.mult, op1=ALU.mult),
                 r=[ps_key, 'gqk', ct_key], w=[c['kt1'], ps_key])
            return c

        def rope_r1b(c):
            sq, kgb = c['sq'], c['kgb']
            S.op('pe', lambda e: e.matmul(bank(5), lhsT=swm[:], rhs=kgb, start=True, stop=True), r=['swm', c['kkgb']], w=[bkey(5)], t=0.4)
            S.op('pe', lambda e: e.matmul(bank(6), lhsT=bo[:], rhs=sq, start=True, stop=True), r=['bo', c['ksq']], w=[bkey(6)], t=0.3)

        def rope_r2(c):
            c['t2'], c['kt2'] = T2.next()
            c['sr'], c['ksr'] = SR.next()
            t1, t2, sr, st_ap = c['t1'], c['t2'], c['sr'], c['st_ap']
            S.op('dve', lambda e: e.tensor_tensor(out=t2, in0=bank(5), in1=st_ap, op=ALU.mult), r=[bkey(5), c['st_key']], w=[c['kt2']])
            S.op('act', lambda e: e.activation(out=sr, in_=bank(6), func=AF.Ln, scale=1.0 / 64.0, bias=EPS), r=[bkey(6)], w=[c['ksr']])
            S.op('dve', lambda e: e.tensor_tensor(out=t1, in0=t1, in1=t2, op=ALU.add), r=[c['kt1'], c['kt2']], w=[c['kt1']])

        def rope_r3(c):
            c['ri'], c['kri'] = RI.next()
            t1, sr, ri, dst_ap = c['t1'], c['sr'], c['ri'], c['dst_ap']
            S.op('act', lambda e: e.activation(out=ri, in_=sr, func=AF.Exp, scale=-0.5), r=[c['ksr']], w=[c['kri']])
            S.op('dve', lambda e: e.tensor_tensor(out=dst_ap, in0=t1, in1=ri, op=ALU.mult), r=[c['kt1'], c['kri']], w=[c['dst_key']])

        def rope_push(*args):
            c = rope_r1a(*args)
            if len(rope_q) >= 1:
                rope_r2(rope_q[-1])
            if len(rope_q) >= 2:
                rope_r3(rope_q[-2])
            rope_r1b(c)
            rope_q.append(c)

        def rope_flush():
            if len(rope_q) >= 1:
                rope_r2(rope_q[-1])
            if len(rope_q) >= 2:
                rope_r3(rope_q[-2])
            if len(rope_q) >= 1:
                rope_r3(rope_q[-1])

        def next_proj():
            b = PROJ_B[proj_i[0] % 3]
            proj_i[0] += 1
            return bank(b), bkey(b)

        ginfo = {}

        def stage_a1(g):
            hb, khb = HB.next()
            ginfo[g] = dict(hb=hb, khb=khb)
            for tt in range(4):
                c = g * 4 + tt
                xt, kx = XT.next()
                dma('sp', xt, x[c * P:(c + 1) * P, :], [], [kx], kx)
                S.op('act', lambda e, xt=xt, c=c, hb=hb, tt=tt: e.activation(out=hb[:, tt, :], in_=xt, func=AF.Square, accum_out=ssall[:, c:c + 1]),
                     r=[kx], w=[(khb, tt), ('ss', c)], t=1.0)
                S.op('dve', lambda e, c=c: e.tensor_scalar(out=msall[:, c:c + 1], in0=ssall[:, c:c + 1], scalar1=1.0 / D, scalar2=EPS,
                                                          op0=ALU.mult, op1=ALU.add), r=[('ss', c)], w=[('ms', c)], t=0.08)
                S.op('pool', lambda e, c=c: e.tensor_tensor(out=rsall[:, c:c + 1], in0=msall[:, c:c + 1], in1=negh[:, 0:1], op=ALU.pow),
                     r=[('ms', c), 'negh'], w=[('rs', c)], t=0.5)
                S.op('dve', lambda e, xt=xt, c=c, tt=tt, hb=hb: e.tensor_scalar(out=hb[:, tt, :], in0=xt, scalar1=rsall[:, c:c + 1], scalar2=None, op0=ALU.mult),
                     r=[kx, ('rs', c)], w=[(khb, tt)], t=0.65)

        def stage_a2(g):
            gi = ginfo[g]
            hb, khb = gi['hb'], gi['khb']
            ct, kct = CT.next()
            stt_, kst = STb.next()
            gi.update(ct=ct, kct=kct, st=stt_, kst=kst)
            dma('sp', ct, ctab[:, g * G:(g + 1) * G], [], [kct], kct)
            dma('sp', stt_, stab[:, g * G:(g + 1) * G], [], [kst], kst)
            if g < NG_OWN:
                htg, khtg = HT[:, g], ('HT', g)
            else:
                htg, khtg = HTG.next()
            gi['htg'], gi['khtg'] = htg, khtg
            for tt in range(4):
                b = tt % 2
                pb = bank(b).bitcast(BF16)
                for kc in range(8):
                    S.op('pe', lambda e, pb=pb, tt=tt, kc=kc, hb=hb: e.transpose(
                        out=pb[:, kc * 128:(kc + 1) * 128], in_=hb[:, tt, kc * 128:(kc + 1) * 128], identity=ident[:]),
                        r=[(khb, tt), 'ident'], w=[bkey(b)], t=0.09)
                dst = htg[:, :, tt * 128:(tt + 1) * 128]
                src = pb.rearrange("p (k t) -> p k t", k=8)
                if evac_flip[0] % 4 != 3:
                    S.op('act', lambda e, dst=dst, src=src: e.activation(out=dst, in_=src, func=AF.Copy), r=[bkey(b)], w=[(khtg, tt)])
                else:
                    S.op('dve', lambda e, dst=dst, src=src: e.tensor_copy(out=dst, in_=src), r=[bkey(b)], w=[(khtg, tt)])
                evac_flip[0] += 1

        def stage_a3(g):
            gi = ginfo[g]
            htg, khtg, ct, kct, stt_, kst = gi['htg'], gi['khtg'], gi['ct'], gi['kct'], gi['st'], gi['kst']
            hkeys = [(khtg, kp) for kp in range(4)]
            pk, kbk = next_proj()
            for kc in range(8):
                S.op('pe', lambda e, pk=pk, kc=kc: e.matmul(pk, lhsT=wkvq[:, kc, 0:128], rhs=htg[:, kc, :], start=(kc == 0), stop=(kc == 7)),
                     r=hkeys + WKV_KEYS, w=[kbk])
            rope_push(pk, kbk, gqk_t[:, 1:2], KT[:, g * G:(g + 1) * G], ('KT', g), ct, kct, stt_, kst)
            pv, kbv = next_proj()
            for tt in range(4):
                for kc in range(8):
                    S.op('pe', lambda e, tt=tt, kc=kc, pv=pv: e.matmul(pv[:, tt * 128:(tt + 1) * 128], lhsT=htg[:, kc, tt * 128:(tt + 1) * 128],
                                                                    rhs=wkvq[:, kc, 128:256], start=(kc == 0), stop=(kc == 7)),
                         r=[(khtg, tt)] + WKV_KEYS, w=[kbv], t=0.12)
            vsrc = pv.rearrange("p (t h d) -> p t h d", t=4, h=2)
            S.op('dve', lambda e: e.tensor_copy(out=Vsb[:, g * 4:(g + 1) * 4, 0:64], in_=vsrc[:, :, 0, :]), r=[kbv], w=[('V0', g)])
            S.op('dve', lambda e: e.tensor_copy(out=Vsb[:, g * 4:(g + 1) * 4, 128:192], in_=vsrc[:, :, 1, :]), r=[kbv], w=[('V1', g)])
            if g < NG_OWN:
                for j in range(4):
                    pq, qbk = next_proj()
                    for kc in range(8):
                        S.op('pe', lambda e, pq=pq, kc=kc, j=j: e.matmul(pq, lhsT=wkvq[:, kc, 256 + j * 128:256 + (j + 1) * 128], rhs=htg[:, kc, :],
                                                                      start=(kc == 0), stop=(kc == 7)),
                             r=hkeys + WQ_KEYS, w=[qbk])
                    rope_push(pq, qbk, gqk_t[:, 0:1], QT[:, j, g * G:(g + 1) * G], ('QT', j, g), ct, kct, stt_, kst)

        for s_ in range(n_a_groups + 2):
            if s_ < n_a_groups:
                stage_a1(s_)
            if s_ == 0:
                issue_wkvq()
            if 0 <= s_ - 1 < n_a_groups:
                stage_a2(s_ - 1)
            if 0 <= s_ - 2 < n_a_groups:
                stage_a3(s_ - 2)
        rope_flush()
        for cb in (1, 0, 2):
            for kh in range(2):
                dma('pool', wa2[:, kh * 4:(kh + 1) * 4, cb * 512:(cb + 1) * 512], w_a2[:, kh * 4:(kh + 1) * 4, cb * 512:(cb + 1) * 512], [('KT', 0)],
                    [('wa2', cb, kh)], ('wa2', cb, kh))
                fold_gain(wa2[:, :, cb * 512:(cb + 1) * 512], range(kh * 4, (kh + 1) * 4), [('wa2', cb, kh)])

        if debug:
            S.barrier()
            dma('sp', dbg["d_kt"], KT, [], ['dbg1'], 'dbg1')
            dma('sp', dbg["d_v"], arena[:, OFF_V // 2:(OFF_V + 24 * KB) // 2], [], ['dbg2'], 'dbg2')
            dma('sp', dbg["d_qt"], arena[:, OFF_QT // 2:(OFF_QT + 16 * KB) // 2], [], ['dbg3'], 'dbg3')
            dma('sp', dbg["d_ht"], arena[:, OFF_HT // 2:(OFF_HT + 32 * KB) // 2], [], ['dbg4'], 'dbg4')
        S.barrier()

        GROUPS = [[0, 1, 2, 3], [4, 5, 6, 7]]
        dma('sp', cc_srcK, KT[:, 0:S_OWN], [('KT', g) for g in range(4)], ['ccsK'], 'ccsK')
        dma('sp', cc_srcV, Vsb[:, 0:16, :], [('V0', g) for g in range(4)] + [('V1', g) for g in range(4)] + ['vones'], ['ccsV'], 'ccsV')
        S.op('pool', lambda e: e.collective_compute("AllGather", ALU.bypass, replica_groups=GROUPS, ins=[cc_srcK.opt()], outs=[cc_dstK.opt()]),
             r=['ccsK'], w=['ccdK'], dma='ccK', inc=1, t=45.0)
        S.op('pool', lambda e: e.collective_compute("AllGather", ALU.bypass, replica_groups=GROUPS, ins=[cc_srcV.opt()], outs=[cc_dstV.opt()]),
             r=['ccsV'], w=['ccdV'], dma='ccV', inc=1, t=65.0)
        for rk in range(4):
            dma('sp', KT[:, rk * S_OWN:(rk + 1) * S_OWN], cc_dstK[rk * P:(rk + 1) * P, :], ['ccdK', 'ccsK'],
                [('KT', rk * 4 + g) for g in range(4)], ('ldK', rk))
        for rk in range(4):
            dma('sp', Vsb[:, rk * 16:(rk + 1) * 16, :], cc_dstV[rk * P:(rk + 1) * P, :].rearrange("p (k c) -> p k c", c=192), ['ccdV', 'ccsV'],
                [('V0', rk * 4 + g) for g in range(4)] + [('V1', rk * 4 + g) for g in range(4)] + ['vones'], ('ldV', rk))

        NGB = min(NG_OWN, n_a_groups) if max_phase >= 2 else 0
        WB = Bump(arena, OFF_ATTN, WA_END)
        wst = WB.alloc(2 * KB, BF16).rearrange("p (g i) -> p g i", g=8)
        lng = WB.alloc(2 * KB, F32)
        lnb = WB.alloc(2 * KB, F32)
        bsp = WB.alloc(8 * KB, F32).rearrange("p (c t) -> p c t", c=4)
        GV = Rot('gv', [WB.alloc(2 * KB, F32) for _ in range(2)])
        VN = Rot('vn', [WB.alloc(2 * KB, F32) for _ in range(2)])
        VTOK = Rot('vtok', [WB.alloc(4 * KB, BF16).rearrange("p (t c) -> p t c", t=4) for _ in range(2)])
        USB = Rot('usb', [WB.alloc(8 * KB, F32).rearrange("p (c t) -> p c t", c=4) for _ in range(2)])
        SGB = Rot('sgb', [WB.alloc(2 * KB, F32) for _ in range(2)])
        TA = Rot('ta', [WB.alloc(2 * KB, F32) for _ in range(2)])
        TB = Rot('tb', [WB.alloc(2 * KB, F32) for _ in range(2)])
        WA2_U = [('wa2', 0, 0), ('wa2', 0, 1)]
        WA2_V = [('wa2', 1, 0), ('wa2', 1, 1)]
        WA2_G = [('wa2', 2, 0), ('wa2', 2, 1)]
        dma('pool', wst, wsT, [], ['wst'], 'c_wst')
        dma('sp', lng, lngv.partition_broadcast(P), [], ['lng'], 'c_lng')
        dma('sp', lnb, lnbv.partition_broadcast(P), [], ['lnb'], 'c_lnb')
        dma('sp', bsp, bspd, [], ['bsp'], 'c_bsp')
        VIN = Rot('vinb', [bank(0), bank(1)])
        UB = Rot('ub', [bank(2), bank(3)])
        for g in range(NGB):
            ht = HT[:, g]
            hk = [('HT', g)]
            vtok, kvtok = VTOK.next()
            usb, kusb = USB.next()
            for tt in range(4):
                c = g * 4 + tt
                pv, _ = VIN.next()
                bk = ('ps', (VIN.i - 1) % 2)
                for kc in range(8):
                    S.op('pe', lambda e, pv=pv, kc=kc, tt=tt, ht=ht: e.matmul(pv, lhsT=ht[:, kc, tt * 128:(tt + 1) * 128], rhs=wa2[:, kc, 512:1024],
                                                                           start=(kc == 0), stop=(kc == 7)), r=hk + WA2_V, w=[bk])
                gv, kgv = GV.next()
                vn, kvn = VN.next()
                S.op('act', lambda e, gv=gv, pv=pv: e.activation(out=gv, in_=pv, func=AF.Gelu_apprx_tanh), r=[bk], w=[kgv])
                S.op('dve', lambda e, gv=gv, c=c: e.bn_stats(out=lnst[:, c * 6:(c + 1) * 6], in_=gv), r=[kgv], w=[('lnst', c)])
                S.op('dve', lambda e, c=c: e.bn_aggr(out=lnmv[:, c * 2:(c + 1) * 2], in_=lnst[:, c * 6:(c + 1) * 6]), r=[('lnst', c)], w=[('lnmv', c)], t=0.08)
                S.op('dve', lambda e, c=c: e.tensor_scalar(out=lnve[:, c:c + 1], in0=lnmv[:, c * 2 + 1:c * 2 + 2], scalar1=EPS, scalar2=None, op0=ALU.add),
                     r=[('lnmv', c)], w=[('lnve', c)], t=0.08)
                S.op('pool', lambda e, c=c: e.tensor_tensor(out=lnrs[:, c:c + 1], in0=lnve[:, c:c + 1], in1=negh[:, 0:1], op=ALU.pow),
                     r=[('lnve', c), 'negh'], w=[('lnrs', c)], t=0.5)
                S.op('dve', lambda e, gv=gv, vn=vn, c=c: e.tensor_scalar(out=vn, in0=gv, scalar1=lnmv[:, c * 2:c * 2 + 1], scalar2=lnrs[:, c:c + 1],
                                                                       op0=ALU.subtract, op1=ALU.mult), r=[kgv, ('lnmv', c), ('lnrs', c)], w=[kvn])
                S.op('pool', lambda e, vn=vn: e.tensor_tensor(out=vn, in0=vn, in1=lng, op=ALU.mult), r=[kvn, 'lng'], w=[kvn])
                S.op('pool', lambda e, vn=vn, vtok=vtok, tt=tt: e.tensor_tensor(out=vtok[:, tt, :], in0=vn, in1=lnb, op=ALU.add), r=[kvn, 'lnb'], w=[(kvtok, tt)])
            for ct_ in range(4):
                pu, _ = UB.next()
                bk = ('ps', 2 + (UB.i - 1) % 2)
                for kc in range(8):
                    S.op('pe', lambda e, pu=pu, kc=kc, ct_=ct_, ht=ht: e.matmul(pu, lhsT=wa2[:, kc, ct_ * 128:(ct_ + 1) * 128], rhs=ht[:, kc, :],
                                                                             start=(kc == 0), stop=(kc == 7)), r=hk + WA2_U, w=[bk])
                S.op('act', lambda e, pu=pu, ct_=ct_, usb=usb: e.activation(out=usb[:, ct_, :], in_=pu, func=AF.Gelu_apprx_tanh), r=[bk], w=[(kusb, ct_)])
            for pr in range(4):
                for tt in range(4):
                    for half in range(2):
                        grp = 2 * pr + half
                        S.op('pe', lambda e, pr=pr, tt=tt, half=half, grp=grp, vtok=vtok: e.matmul(
                            bank(4 + pr)[half * 64:(half + 1) * 64, tt * 128:(tt + 1) * 128], lhsT=vtok[:, tt, grp * 64:(grp + 1) * 64],
                            rhs=wst[:, grp, :], start=True, stop=True), r=[(kvtok, tt), 'wst'], w=[bkey(4 + pr)], t=0.06)
            for ct_ in range(4):
                pgb, _ = VIN.next()
                bk = ('ps', (VIN.i - 1) % 2)
                for kc in range(8):
                    S.op('pe', lambda e, pgb=pgb, kc=kc, ct_=ct_, ht=ht: e.matmul(pgb, lhsT=wa2[:, kc, 1024 + ct_ * 128:1024 + (ct_ + 1) * 128], rhs=ht[:, kc, :],
                                                                               start=(kc == 0), stop=(kc == 7)), r=hk + WA2_G, w=[bk])
                sgb, ksgb = SGB.next()
                ta, kta = TA.next()
                tb, ktb = TB.next()
                S.op('act', lambda e, sgb=sgb, pgb=pgb: e.activation(out=sgb, in_=pgb, func=AF.Silu), r=[bk], w=[ksgb])
                S.op('dve', lambda e, ta=ta, ct_=ct_: e.tensor_tensor(out=ta, in0=bank(4 + ct_), in1=bsp[:, ct_, :], op=ALU.add), r=[bkey(4 + ct_), 'bsp'], w=[kta])
                S.op('pool', lambda e, tb=tb, sgb=sgb, ct_=ct_, usb=usb: e.tensor_tensor(out=tb, in0=usb[:, ct_, :], in1=sgb, op=ALU.mult), r=[(kusb, ct_), ksgb], w=[ktb])
                S.op('dve', lambda e, ta=ta, tb=tb, ct_=ct_, g=g: e.tensor_tensor(out=GM[:, ct_, g * G:(g + 1) * G], in0=ta, in1=tb, op=ALU.mult), r=[kta, ktb], w=[('GM', ct_, g)])
        if debug:
            S.barrier()
            dma('sp', dbg["d_gm"], arena[:, OFF_GM // 2:(OFF_GM + 16 * KB) // 2], [], ['dbg5'], 'dbg5')
        S.barrier()

        S.reorder = False
        WC = Bump(arena, OFF_FLEX, ARENA)
        wga = WC.alloc(8 * KB, BF16).rearrange("p (k n) -> p k n", k=8)
        wpa = WC.alloc(8 * KB, BF16).rearrange("p (k n) -> p k n", k=4)
        wg = WC.alloc(32 * KB, BF16).rearrange("p (k n) -> p k n", k=8)
        wout = WC.alloc(16 * KB, BF16).rearrange("p (k n) -> p k n", k=8)
        WC_WORK = WC.off
        PT = Rot('pt', [WC.alloc(2 * KB, BF16) for _ in range(4)])
        OSB = [WC.alloc(2 * KB, F32) for _ in range(2)]
        _rs = WC.alloc(2 * KB, F32)
        _rs2 = WC.alloc(2 * KB, F32)
        RS = [_rs, _rs]
        RS2 = [_rs2, _rs2]
        for kc in range(0, 8, 2):
            dma('pool', wg[:, kc:kc + 2, :], w_g[:, kc:kc + 2, :], [], [('wg', kc)], ('wg', kc))
            fold_gain(wg, [kc, kc + 1], [('wg', kc)])
        WG_KEYS = [('wg', kc) for kc in range(0, 8, 2)]
        for kc in range(0, 8, 4):
            dma('pool', wout[:, kc:kc + 4, :], w_o[:, kc:kc + 4, :], [], [('wout', kc)], ('wout', kc))
        WOUT_KEYS = [('wout', 0), ('wout', 4)]
        dma('pool', wga, w_ga, [], ['wga'], 'c_wga')
        fold_gain(wga, range(8), ['wga'])
        dma('pool', wpa, w_pa, [], ['wpa'], 'c_wpa')

        SC = 0.125
        OBK = 6
        passes = [(qg, j) for qg in range(NG_OWN if max_phase >= 3 else 0) for j in range(4)]
        NT = len(passes) * 64

        def qkmm(t):
            qg, j = passes[t // 64]
            kt = t % 64
            sb_ = t % 3
            qA = QT[0:64, j, qg * G:(qg + 1) * G]
            qB = QT[64:128, j, qg * G:(qg + 1) * G]
            rk = [('KT', kt // 4), ('QT', j, qg)]
            S.op('pe', lambda e: e.matmul(PS[sb_][:, 0:512], lhsT=KT[0:64, kt * 128:(kt + 1) * 128], rhs=qA, start=True, stop=True),
                 r=rk, w=[bkey(2 * sb_)])
            S.op('pe', lambda e: e.matmul(PS[sb_][:, 512:1024], lhsT=KT[64:128, kt * 128:(kt + 1) * 128], rhs=qB, start=True, stop=True),
                 r=rk, w=[bkey(2 * sb_ + 1)])

        def expo(t):
            sb_ = t % 3
            pt, kpt = PT.next()
            S.op('act', lambda e: e.activation(out=pt, in_=PS[sb_][:, :], func=AF.Exp, scale=SC, bias=-12.0),
                 r=[bkey(2 * sb_), bkey(2 * sb_ + 1)], w=[kpt], t=1.0)
            return pt, kpt

        def normalise(qg, j, last=False):
            S.op('dve', lambda e: e.tensor_copy(out=OSB[0], in_=bank(OBK)), r=[bkey(OBK)], w=['osb0'])
            S.op('dve', lambda e: e.tensor_copy(out=OSB[1], in_=bank(OBK + 1)), r=[bkey(OBK + 1)], w=['osb1'])
            S.op('dve', lambda e: e.reciprocal(out=RS[0][64:128, :], in_=OSB[0][64:128, :]), r=['osb0'], w=['rs0'])
            if last:
                S.op('act', lambda e: e.activation(out=RS[1][0:64, :], in_=OSB[1][0:64, :], func=AF.Ln), r=['osb1'], w=['rs1'])
                S.op('act', lambda e: e.activation(out=RS[1][0:64, :], in_=RS[1][0:64, :], func=AF.Exp, scale=-1.0), r=['rs1'], w=['rs1'])
            else:
                S.op('dve', lambda e: e.reciprocal(out=RS[1][0:64, :], in_=OSB[1][0:64, :]), r=['osb1'], w=['rs1'])
            S.op('dve', lambda e: e.tensor_copy(out=RS2[0][0:64, :], in_=RS[0][64:128, :]), r=['rs0'], w=['rs20'])
            S.op('dve', lambda e: e.tensor_copy(out=RS2[1][64:128, :], in_=RS[1][0:64, :]), r=['rs1'], w=['rs21'])
            S.op('dve', lambda e: e.tensor_tensor(out=ATT[0:64, j, qg * G:(qg + 1) * G], in0=OSB[0][0:64, :], in1=RS2[0][0:64, :], op=ALU.mult),
                 r=['osb0', 'rs20'], w=[('ATT0', j, qg)])
            S.op('dve', lambda e: e.tensor_tensor(out=ATT[64:128, j, qg * G:(qg + 1) * G], in0=OSB[1][64:128, :], in1=RS2[1][64:128, :], op=ALU.mult),
                 r=['osb1', 'rs21'], w=[('ATT1', j, qg)])

        def pvmm(t, pt, kpt):
            qg, j = passes[t // 64]
            kt = t % 64
            vk = [('V0', kt // 4), ('V1', kt // 4), 'vones']
            S.op('pe', lambda e: e.matmul(bank(OBK), lhsT=Vsb[:, kt, 0:128], rhs=pt[:, 0:512], start=(kt == 0), stop=(kt == 63)),
                 r=[kpt] + vk, w=[bkey(OBK)])
            S.op('pe', lambda e: e.matmul(bank(OBK + 1), lhsT=Vsb[:, kt, 64:192], rhs=pt[:, 512:1024], start=(kt == 0), stop=(kt == 63)),
                 r=[kpt] + vk, w=[bkey(OBK + 1)])
            if kt == 63:
                normalise(qg, j, last=(t == NT - 1))

        if NT:
            qkmm(0)
            qkmm(1)
        for t in range(0, NT, 2):
            p0 = expo(t)
            p1 = expo(t + 1)
            if t + 2 < NT:
                qkmm(t + 2)
                qkmm(t + 3)
            pvmm(t, *p0)
            pvmm(t + 1, *p1)
        hoisted = {}
        if NT and max_phase >= 4:
            for j in range(2):
                for kc in range(8):
                    S.op('pe', lambda e, kc=kc, j=j: e.matmul(bank(j), lhsT=wga[:, kc, j * 128:(j + 1) * 128], rhs=HT[:, 0][:, kc, :],
                                                           start=(kc == 0), stop=(kc == 7)), r=[('HT', 0), 'wga'], w=[bkey(j)])
                hoisted[(0, j)] = True
        if debug:
            S.barrier()
            dma('sp', dbg["d_attn"], arena[:, OFF_ATTN // 2:(OFF_ATTN + 16 * KB) // 2], [], ['dbg6'], 'dbg6')
        S.barrier()

        S.reorder = True
        WD = Bump(arena, OFF_KT, OFF_HT)
        wpg = WD.alloc(8 * KB, BF16).rearrange("p (k n) -> p k n", k=4)
        fgt = WD.alloc(4 * KB, F32)
        WD2 = Bump(arena, WC_WORK, ARENA)
        XC = Rot('xc', [WD2.alloc(4 * KB, F32) for _ in range(2)] + [WD.alloc(4 * KB, F32)])
        SGA = Rot('sga', [WD.alloc(2 * KB, F32) for _ in range(2)])
        AT = Rot('aT', [WD.alloc(4 * KB, BF16).rearrange("p (j t) -> p j t", j=4) for _ in range(2)])
        G0 = Rot('g0', [WD.alloc(2 * KB, F32) for _ in range(2)])
        G1 = Rot('g1', [WD.alloc(2 * KB, F32) for _ in range(2)])
        TC = Rot('tc', [WD.alloc(2 * KB, F32) for _ in range(2)])
        TD = Rot('td', [WD.alloc(2 * KB, F32) for _ in range(2)])
        YT = Rot('yT', [WD.alloc(8 * KB, BF16).rearrange("p (k t) -> p k t", k=8) for _ in range(1)])
        RR = Rot('rr', [WD2.alloc(4 * KB, F32) for _ in range(2)] + [WD.alloc(4 * KB, F32)])
        dma('pool', wpg, w_pg, [], ['wpg'], 'c_wpg')
        dma('sp', fgt, fgv.partition_broadcast(P), [], ['fgt'], 'c_fgt')
        GAB = Rot('gab', [bank(0), bank(1)])
        tails = []

        def emit_tail(rr, krr, c):
            S.op('dve', lambda e: e.scalar_tensor_tensor(out=rr, in0=rr, scalar=frs[:, c:c + 1], in1=fgt, op0=ALU.mult, op1=ALU.mult),
                 r=[(krr, 0), (krr, 1), ('frs', c), 'fgt'], w=[(krr, 0), (krr, 1)], t=1.3)
            dma('sp', out[c * P:(c + 1) * P, :], rr, [(krr, 0), (krr, 1)], [('out', c)], ('o', c % 3))
        OB = Rot('ob', [bank(6), bank(7)])
        for g in range(NG_OWN if max_phase >= 4 else 0):
            ht = HT[:, g]
            hk = [('HT', g)]
            aT, kaT = AT.next()
            yT, kyT = YT.next()
            for j in range(4):
                pga, _ = GAB.next()
                bk = ('ps', (GAB.i - 1) % 2)
                for kc in range(8):
                    if (g, j) in hoisted:
                        continue
                    S.op('pe', lambda e, pga=pga, kc=kc, j=j, ht=ht: e.matmul(pga, lhsT=wga[:, kc, j * 128:(j + 1) * 128], rhs=ht[:, kc, :],
                                                                           start=(kc == 0), stop=(kc == 7)), r=hk + ['wga'], w=[bk])
                sga, ksga = SGA.next()
                S.op('act', lambda e, sga=sga, pga=pga: e.activation(out=sga, in_=pga, func=AF.Silu), r=[bk], w=[ksga])
                S.op('dve', lambda e, sga=sga, j=j, g=g, aT=aT: e.tensor_tensor(out=aT[:, j, :], in0=sga, in1=ATT[:, j, g * G:(g + 1) * G], op=ALU.mult),
                     r=[ksga, ('ATT0', j, g), ('ATT1', j, g)], w=[(kaT, j)])
            for ot in range(8):
                g0, kg0 = G0.next()
                g1, kg1 = G1.next()
                tc, ktc = TC.next()
                td, ktd = TD.next()
                for kc in range(8):
                    S.op('pe', lambda e, ot=ot, kc=kc, ht=ht: e.matmul(bank(4), lhsT=wg[:, kc, ot * 128:(ot + 1) * 128], rhs=ht[:, kc, :], start=(kc == 0), stop=(kc == 7)),
                         r=hk + WG_KEYS, w=[bkey(4)])
                S.op('act', lambda e, g0=g0, ot=ot: e.activation(out=g0, in_=bank(4), func=AF.Sigmoid, bias=bm_t[:, ot:ot + 1]), r=[bkey(4), 'bm'], w=[kg0])
                for kc in range(8):
                    S.op('pe', lambda e, ot=ot, kc=kc, ht=ht: e.matmul(bank(5), lhsT=wg[:, kc, 1024 + ot * 128:1024 + (ot + 1) * 128], rhs=ht[:, kc, :],
                                                                    start=(kc == 0), stop=(kc == 7)), r=hk + WG_KEYS, w=[bkey(5)])
                S.op('act', lambda e, g1=g1, ot=ot: e.activation(out=g1, in_=bank(5), func=AF.Sigmoid, bias=bm_t[:, 8 + ot:9 + ot]), r=[bkey(5), 'bm'], w=[kg1])
                for j in range(4):
                    S.op('pe', lambda e, ot=ot, j=j, aT=aT: e.matmul(bank(2), lhsT=wpa[:, j, ot * 128:(ot + 1) * 128], rhs=aT[:, j, :], start=(j == 0), stop=(j == 3)),
                         r=['wpa', (kaT, j)], w=[bkey(2)])
                S.op('dve', lambda e, tc=tc, g0=g0: e.tensor_tensor(out=tc, in0=bank(2), in1=g0, op=ALU.mult), r=[bkey(2), kg0], w=[ktc])
                for c4 in range(4):
                    S.op('pe', lambda e, ot=ot, c4=c4, g=g: e.matmul(bank(3), lhsT=wpg[:, c4, ot * 128:(ot + 1) * 128], rhs=GM[:, c4, g * G:(g + 1) * G],
                                                                  start=(c4 == 0), stop=(c4 == 3)), r=['wpg', ('GM', c4, g)], w=[bkey(3)])
                S.op('dve', lambda e, td=td, g1=g1: e.tensor_tensor(out=td, in0=bank(3), in1=g1, op=ALU.mult), r=[bkey(3), kg1], w=[ktd])
                S.op('pool', lambda e, tc=tc, td=td, ot=ot, yT=yT: e.tensor_tensor(out=yT[:, ot, :], in0=tc, in1=td, op=ALU.add), r=[ktc, ktd], w=[(kyT, ot)])
            ykeys = [(kyT, ot) for ot in range(8)]
            for tt in range(4):
                c = g * 4 + tt
                xc, kxc = XC.next()
                rr, krr = RR.next()
                dma('sp', xc, x[c * P:(c + 1) * P, :], [], [kxc], kxc)
                for half in range(2):
                    po, _ = OB.next()
                    bk = ('ps', 6 + (OB.i - 1) % 2)
                    for ot in range(8):
                        S.op('pe', lambda e, po=po, ot=ot, tt=tt, half=half, yT=yT: e.matmul(po, lhsT=yT[:, ot, tt * 128:(tt + 1) * 128],
                                                                                          rhs=wout[:, ot, half * 512:(half + 1) * 512], start=(ot == 0), stop=(ot == 7)),
                             r=ykeys + WOUT_KEYS, w=[bk])
                    S.op('dve', lambda e, po=po, half=half, rr=rr, xc=xc: e.tensor_tensor(out=rr[:, half * 512:(half + 1) * 512], in0=po,
                                                                                       in1=xc[:, half * 512:(half + 1) * 512], op=ALU.add),
                         r=[bk, kxc], w=[(krr, half)])
                S.op('act', lambda e, rr=rr, c=c, xc=xc: e.activation(out=xc, in_=rr, func=AF.Square, accum_out=fss[:, c:c + 1]),
                     r=[(krr, 0), (krr, 1)], w=[kxc, ('fss', c)], t=1.0)
                S.op('dve', lambda e, c=c: e.tensor_scalar(out=fms[:, c:c + 1], in0=fss[:, c:c + 1], scalar1=1.0 / D, scalar2=EPS, op0=ALU.mult, op1=ALU.add),
                     r=[('fss', c)], w=[('fms', c)], t=0.08)
                S.op('pool', lambda e, c=c: e.tensor_tensor(out=frs[:, c:c + 1], in0=fms[:, c:c + 1], in1=negh[:, 0:1], op=ALU.pow),
                     r=[('fms', c), 'negh'], w=[('frs', c)], t=0.5)
                tails.append((rr, krr, c))
                if len(tails) >= 2:
                    emit_tail(*tails.pop(0))
        while tails:
            emit_tail(*tails.pop(0))
        S.barrier()
        S.emit(nc, st)
    return nc, S


def _rope_tables(r):
    pos = np.arange(S_OWN) + r * S_OWN
    row = (pos // 64).astype(np.float64)
    col = (pos % 64).astype(np.float64)
    inv = 10000.0 ** (-(np.arange(0, 32, 2, dtype=np.float64) / 32.0))
    ct = np.zeros((P, S_OWN), np.float32)
    sn = np.zeros((P, S_OWN), np.float32)
    for p in range(P):
        d = p % 64
        idx = row if d < 32 else col
        dd = d % 32
        f = dd % 16
        ang = idx * inv[f]
        ct[p] = np.cos(ang)
        sn[p] = -np.sin(ang) if dd < 16 else np.sin(ang)
    return ct, sn


def _tile_rows(w, nk):
    return np.ascontiguousarray(w.reshape(nk, P, w.shape[1]).transpose(1, 0, 2))


def prepare_inputs(x, norm_gain, w_in, q_gain, k_gain, w_proj_attn, ln_v_gain, ln_v_bias,
                   w_spatial, b_spatial, w_proj_gmlp, b_merge, w_out, final_gain):
    f = lambda a: np.ascontiguousarray(np.asarray(a, dtype=np.float32))
    x = f(x); w_in = f(w_in)[0]
    wq, wk, wv, wga, wu, wvi, wgb, wgm = np.split(w_in, np.cumsum([512, 128, 128, 512, 512, 512, 512])[:], axis=1)
    pair_cols = np.concatenate([np.r_[j * 64:(j + 1) * 64, (4 + j) * 64:(5 + j) * 64] for j in range(4)])
    shared = {
        "w_kvq": _tile_rows(np.concatenate([wk, wv, wq[:, pair_cols]], axis=1), 8),
        "w_a2": _tile_rows(np.concatenate([wu, wvi, wgb], axis=1), 8),
        "w_ga": _tile_rows(wga[:, pair_cols], 8),
        "w_g": _tile_rows(wgm, 8),
        "w_pa": _tile_rows(f(w_proj_attn)[0][pair_cols, :], 4),
        "w_pg": _tile_rows(f(w_proj_gmlp)[0], 4),
        "w_o": _tile_rows(f(w_out)[0], 8),
        "wsT": np.ascontiguousarray(f(w_spatial)[0].transpose(2, 0, 1)),
        "ng": f(norm_gain)[0],
        "ng_t": np.ascontiguousarray(f(norm_gain)[0].reshape(8, P).T),
        "gqk": np.ascontiguousarray(np.stack([np.tile(f(q_gain)[0], 2), np.tile(f(k_gain)[0], 2)], axis=1)),
        "gq_row": f(q_gain)[0],
        "gk_row": f(k_gain)[0],
        "lng": f(ln_v_gain)[0],
        "lnb": f(ln_v_bias)[0],
        "bm": np.ascontiguousarray(f(b_merge)[0].reshape(16, P).T),
        "fg": f(final_gain),
        "ident": np.eye(P, dtype=np.float32),
    }
    bs = f(b_spatial)[0]
    bsp = np.zeros((P, 4, 4, 128), np.float32)
    for pr in range(4):
        bsp[0:64, pr, :, :] = bs[2 * pr][None, None, :]
        bsp[64:128, pr, :, :] = bs[2 * pr + 1][None, None, :]
    shared["bsp"] = np.ascontiguousarray(bsp.reshape(P, 4, 512))
    partner = np.array([(p // 32) * 32 + ((p % 32) + 16) % 32 for p in range(P)])
    swm = np.zeros((P, P), np.float32)
    swm[partner, np.arange(P)] = 1.0
    shared["swm"] = swm
    bo = np.zeros((P, P), np.float32)
    bo[0:64, 0:64] = 1.0
    bo[64:128, 64:128] = 1.0
    shared["bo"] = bo
    tabs = [_rope_tables(r) for r in range(4)]
    in_maps = []
    for c in range(8):
        b, r = c // 4, c % 4
        m = dict(shared)
        m["x"] = np.ascontiguousarray(x[b, r * S_OWN:(r + 1) * S_OWN])
        m["ctab"], m["stab"] = tabs[r]
        in_maps.append(m)
    return in_maps


_CACHE = {}


def kernel(**inputs):
    in_maps = prepare_inputs(**inputs)
    if "nc" not in _CACHE:
        _CACHE["nc"] = build_program(False)[0]
    nc = _CACHE["nc"]
    res = run_bass_kernel_spmd(nc, in_maps, core_ids=list(range(8)))
    outp = np.zeros((2, S_ALL, D), np.float32)
    for c in range(8):
        b, r = c // 4, c % 4
        outp[b, r * S_OWN:(r + 1) * S_OWN, :] = np.asarray(res.results[c]["out"], dtype=np.float32)
    return outp
```

```python
import contextlib
import numpy as np
import concourse.bass as bass
import concourse.mybir as mybir
from concourse.bass_utils import run_bass_kernel_spmd

F32 = mybir.dt.float32
BF16 = mybir.dt.bfloat16
AF = mybir.ActivationFunctionType
ALU = mybir.AluOpType

P = 128
D = 1024
S_ALL = 8192
S_OWN = 2048
NG_ALL = 16
NG_OWN = 4
G = 512
EPS = 1e-6
ENGS = ('pe', 'act', 'dve', 'pool', 'sp')


class Sched:
    DEF_T = dict(pe=0.25, act=0.7, dve=0.7, pool=1.3, sp=0.05)

    def __init__(self):
        self.ops = []
        self.lastw = {}
        self.readers = {}
        self.marks = []
        self.reorder = True

    def op(self, eng, fn, r=(), w=(), dma=None, t=None, inc=16):
        i = len(self.ops)
        raw, other = set(), set()
        for b in r:
            lw = self.lastw.get(b)
            if lw is not None:
                raw.add(lw)
        for b in w:
            lw = self.lastw.get(b)
            if lw is not None:
                other.add(lw)
            other.update(self.readers.get(b, ()))
        for b in r:
            self.readers.setdefault(b, []).append(i)
        for b in w:
            self.lastw[b] = i
            self.readers[b] = []
        raw.discard(i)
        other.discard(i)
        if t is None:
            t = self.DEF_T[eng]
        self.ops.append(dict(eng=eng, fn=fn, raw=raw, other=other - raw, dma=dma, t=t, inc=inc))
        return i

    def barrier(self):
        self.marks.append((len(self.ops), self.reorder))

    def _schedule(self, idxs):
        ops = self.ops
        iset = set(idxs)
        preds = {i: set(d for d in (ops[i]['raw'] | ops[i]['other']) if d in iset) for i in idxs}
        last_stream = {}
        for i in idxs:
            k = ops[i]['dma']
            if k is not None:
                if k in last_stream:
                    preds[i].add(last_stream[k])
                last_stream[k] = i
        succs = {i: [] for i in idxs}
        indeg = {}
        for i in idxs:
            indeg[i] = len(preds[i])
            for d in preds[i]:
                succs[d].append(i)
        HOP = 0.15
        rtime = {i: 0.0 for i in idxs}
        ready = {e: [] for e in ENGS}
        for i in idxs:
            if indeg[i] == 0:
                ready[ops[i]['eng']].append(i)
        free = {e: 0.0 for e in ENGS}
        order = []
        n = len(idxs)
        while len(order) < n:
            best = None
            for e in ENGS:
                for i in ready[e]:
                    key = (max(rtime[i], free[e]), i)
                    if best is None or key < best[0]:
                        best = (key, i, e)
            (st_, _), i, e = best
            ready[e].remove(i)
            o = ops[i]
            if o['dma'] is not None:
                free[e] = st_ + (0.05 if e == 'sp' else 0.6)
                fin = st_ + 2.0 + o['t']
            else:
                free[e] = st_ + o['t']
                fin = free[e]
            order.append(i)
            for j in succs[i]:
                rtime[j] = max(rtime[j], fin + HOP)
                indeg[j] -= 1
                if indeg[j] == 0:
                    ready[ops[j]['eng']].append(j)
        return order

    def emit(self, nc, stack):
        ops = self.ops
        seq = []
        start = 0
        for (m, reorder) in self.marks:
            idxs = list(range(start, m))
            seq += [('op', i) for i in (self._schedule(idxs) if reorder else idxs)]
            seq.append(('bar',))
            start = m
        assert start == len(ops), "program must end with a barrier"
        for i, o in enumerate(ops):
            deps = set()
            for d in (o['raw'] | o['other']):
                p = ops[d]
                if p['dma'] is None and o['dma'] is None and p['eng'] == o['eng'] and o['eng'] == 'pe':
                    continue
                deps.add(d)
            o['deps'] = deps
        has_dep = [False] * len(ops)
        for o in ops:
            for d in o['deps']:
                has_dep[d] = True
        last = {}
        final = []
        for ent in seq:
            if ent[0] == 'op':
                i = ent[1]
                o = ops[i]
                k = ('dma', o['dma']) if o['dma'] is not None else ('eng', o['eng'])
                last[k] = i
                final.append(ent)
            else:
                dl = set(last.values())
                for d in dl:
                    has_dep[d] = True
                for e in ENGS:
                    final.append(('wait', e, dl))
        eng_sem, eng_cnt, dma_sem, dma_cnt = {}, {}, {}, {}
        for ent in final:
            if ent[0] != 'op':
                continue
            i = ent[1]
            o = ops[i]
            o['sig'] = None
            if o['dma'] is not None:
                k = o['dma']
                if k not in dma_sem:
                    dma_sem[k] = stack.enter_context(nc.semaphore("d%d" % len(dma_sem)))
                    dma_cnt[k] = 0
                dma_cnt[k] += o['inc']
                o['sig'] = (dma_sem[k], dma_cnt[k])
            else:
                e = o['eng']
                if e not in eng_sem:
                    eng_sem[e] = stack.enter_context(nc.semaphore("e_" + e))
                    eng_cnt[e] = 0
                if has_dep[i]:
                    eng_cnt[e] += 1
                    o['sig'] = (eng_sem[e], eng_cnt[e])
        self.stats = dict(n_ops=len(ops), sems=len(eng_sem) + len(dma_sem), eng_cnt=dict(eng_cnt))
        block = stack.enter_context(nc.Block())

        def run(ename, eng):
            known = {}

            def waits(depset):
                need = {}
                for d in depset:
                    sig = ops[d]['sig']
                    assert sig is not None
                    s, v = sig
                    if id(s) not in need or need[id(s)][1] < v:
                        need[id(s)] = (s, v)
                for key, (s, v) in need.items():
                    if known.get(key, 0) >= v:
                        continue
                    eng.wait_ge(s, v)
                    known[key] = v

            for ent in final:
                if ent[0] == 'wait':
                    if ent[1] == ename:
                        waits(ent[2])
                    continue
                o = ops[ent[1]]
                if o['eng'] != ename:
                    continue
                waits(o['deps'])
                ins = o['fn'](eng)
                if o['sig'] is not None:
                    ins.then_inc(o['sig'][0], o['inc'] if o['dma'] is not None else 1)

        @block.tensor
        def _(e):
            run('pe', e)

        @block.scalar
        def _(e):
            run('act', e)

        @block.vector
        def _(e):
            run('dve', e)

        @block.gpsimd
        def _(e):
            run('pool', e)

        @block.sync
        def _(e):
            run('sp', e)


class Rot:
    def __init__(self, name, aps):
        self.name, self.aps, self.i = name, aps, 0

    def next(self):
        k = self.i % len(self.aps)
        self.i += 1
        return self.aps[k], (self.name, k)


class Bump:
    def __init__(self, arena, start, end):
        self.arena, self.off, self.end = arena, start, end

    def alloc(self, nbytes, dt, shape=None):
        nbytes = (nbytes + 63) // 64 * 64
        assert self.off + nbytes <= self.end, (self.off, nbytes, self.end)
        ap = self.arena[:, self.off // 2:(self.off + nbytes) // 2]
        self.off += nbytes
        if dt != BF16:
            ap = ap.bitcast(dt)
        return ap


KB = 1024
OFF_KT, OFF_V, OFF_QT, OFF_HT, OFF_GM, OFF_ATTN, OFF_FLEX, ARENA = (
    0, 16 * KB, 40 * KB, 56 * KB, 88 * KB, 104 * KB, 120 * KB, 200 * KB)


def build_program(debug=False, max_phase=4, n_a_groups=NG_OWN):
    nc = bass.Bass("TRN2", target_bir_lowering=False)

    def din(name, shape, dt=F32):
        return nc.dram_tensor(name, shape, dt, kind="ExternalInput").ap()

    x = din("x", [S_OWN, D])
    ctab = din("ctab", [P, S_OWN])
    stab = din("stab", [P, S_OWN])
    cc_srcK = nc.dram_tensor("cc_srcK", [P, S_OWN], BF16).ap()
    cc_dstK = nc.dram_tensor("cc_dstK", [4 * P, S_OWN], BF16).ap()
    cc_srcV = nc.dram_tensor("cc_srcV", [P, 16 * 192], BF16).ap()
    cc_dstV = nc.dram_tensor("cc_dstV", [4 * P, 16 * 192], BF16).ap()
    w_kvq = din("w_kvq", [P, 8, 768])
    w_a2 = din("w_a2", [P, 8, 1536])
    w_ga = din("w_ga", [P, 8, 512])
    w_g = din("w_g", [P, 8, 2048])
    w_pa = din("w_pa", [P, 4, 1024])
    w_pg = din("w_pg", [P, 4, 1024])
    w_o = din("w_o", [P, 8, 1024])
    wsT = din("wsT", [P, 8, 128])
    ngv = din("ng", [D])
    ngtd = din("ng_t", [P, 8])
    gqk = din("gqk", [P, 2])
    gq_row = din("gq_row", [64])
    gk_row = din("gk_row", [64])
    lngv = din("lng", [512])
    lnbv = din("lnb", [512])
    bspd = din("bsp", [P, 4, 512])
    bmd = din("bm", [P, 16])
    fgv = din("fg", [D])
    identd = din("ident", [P, P])
    swmd = din("swm", [P, P])
    bod = din("bo", [P, P])
    out = nc.dram_tensor("out", [S_OWN, D], F32, kind="ExternalOutput").ap()
    dbg = {}
    if debug:
        for nm, shp, dt in [("d_kt", [P, S_ALL], BF16), ("d_v", [P, 64 * 192], BF16), ("d_qt", [P, 4 * S_OWN], BF16),
                            ("d_ht", [P, 4 * 8 * 512], BF16), ("d_gm", [P, 4 * S_OWN], BF16),
                            ("d_attn", [P, 4 * S_OWN], BF16)]:
            dbg[nm] = nc.dram_tensor(nm, shp, dt, kind="ExternalOutput").ap()

    S = Sched()
    with contextlib.ExitStack() as st:
        arena = st.enter_context(nc.sbuf_tensor("arena", [P, ARENA // 2], BF16))

        def sb(name, shape, dt=F32):
            return st.enter_context(nc.sbuf_tensor("s_" + name, shape, dt))

        PS = [st.enter_context(nc.psum_tensor("ps%d" % i, [P, 1024], F32)) for i in range(4)]

        def bank(b):
            return PS[b // 2][:, (b % 2) * 512:(b % 2) * 512 + 512]

        def bkey(b):
            return ('ps', b)

        ident = sb("ident", [P, P], BF16)
        swm = sb("swm", [P, P], BF16)
        bo = sb("bo", [P, P], BF16)
        gqk_t = sb("gqk", [P, 2], F32)
        bm_t = sb("bm", [P, 16], F32)
        negh = sb("negh", [P, 1], F32)
        ngT = sb("ngT", [P, 8], F32)
        gq_b = sb("gq_b", [P, 64], F32)
        gk_b = sb("gk_b", [P, 64], F32)
        mqk = sb("mqk", [P, 2], F32)
        nbias = sb("nbias", [P, 1], F32)
        ssall = sb("ssall", [P, 64], F32)
        msall = sb("msall", [P, 64], F32)
        rsall = sb("rsall", [P, 64], F32)
        fss = sb("fss", [P, 16], F32)
        fms = sb("fms", [P, 16], F32)
        frs = sb("frs", [P, 16], F32)
        lnst = sb("lnst", [P, 16 * 6], F32)
        lnmv = sb("lnmv", [P, 16 * 2], F32)
        lnve = sb("lnve", [P, 16], F32)
        lnrs = sb("lnrs", [P, 16], F32)

        KT = arena[:, OFF_KT // 2:(OFF_KT + 16 * KB) // 2]
        Vsb = arena[:, OFF_V // 2:(OFF_V + 24 * KB) // 2].rearrange("p (k c) -> p k c", c=192)
        QT = arena[:, OFF_QT // 2:(OFF_QT + 16 * KB) // 2].rearrange("p (j t) -> p j t", j=4)
        HT = arena[:, OFF_HT // 2:(OFF_HT + 32 * KB) // 2].rearrange("p (g k t) -> p g k t", g=4, k=8)
        GM = arena[:, OFF_GM // 2:(OFF_GM + 16 * KB) // 2].rearrange("p (c t) -> p c t", c=4)
        ATT = arena[:, OFF_ATTN // 2:(OFF_ATTN + 16 * KB) // 2].rearrange("p (j t) -> p j t", j=4)

        def dma(eng, out_ap, in_ap, r, w, stream):
            S.op(eng, lambda e: e.dma_start(out=out_ap, in_=in_ap), r=r, w=w, dma=stream)

        dma('pool', ident[:], identd, [], ['ident'], 'c_ident')
        dma('pool', swm[:], swmd, [], ['swm'], 'c_swm')
        dma('pool', bo[:], bod, [], ['bo'], 'c_bo')

        def fold_gain(w_ap, kcs, keys, eng='dve'):
            for kc in kcs:
                S.op(eng, lambda e, kc=kc: e.tensor_scalar(out=w_ap[:, kc, :], in0=w_ap[:, kc, :], scalar1=ngT[:, kc:kc + 1], scalar2=None, op0=ALU.mult),
                     r=list(keys) + ['ngT'], w=list(keys), t=0.3)
        S.op('pool', lambda e: e.memset(negh[:], -0.5), w=['negh'])
        S.op('pool', lambda e: e.memset(Vsb[:, 0:16, 64:128], 1.0), w=['vones'], t=1.5)

        WA_END = OFF_FLEX + 56 * KB
        WA = Bump(arena, OFF_GM, WA_END)
        wa2 = arena[:, WA_END // 2:ARENA // 2].rearrange("p (k n) -> p k n", k=8)
        wkvq = WA.alloc(12 * KB, BF16).rearrange("p (k n) -> p k n", k=8)
        XT = Rot('xt', [WA.alloc(4 * KB, F32) for _ in range(4)])
        HB = Rot('hb', [WA.alloc(8 * KB, BF16).rearrange("p (t d) -> p t d", t=4) for _ in range(3)])
        CT = Rot('ct', [WA.alloc(2 * KB, F32) for _ in range(3)])
        STb = Rot('st', [WA.alloc(2 * KB, F32) for _ in range(3)])
        HTG = None
        SQ = Rot('sq', [WA.alloc(1 * KB, BF16) for _ in range(2)])
        KGB = Rot('kgb', [WA.alloc(1 * KB, BF16) for _ in range(2)])
        T1 = Rot('t1', [WA.alloc(2 * KB, F32) for _ in range(3)])
        T2 = Rot('t2', [WA.alloc(2 * KB, F32) for _ in range(2)])
        SR = Rot('sr', [WA.alloc(2 * KB, F32) for _ in range(2)])
        RI = Rot('ri', [WA.alloc(2 * KB, F32) for _ in range(2)])

        WKV_KEYS = [('wkvq', 'kv', 0), ('wkvq', 'kv', 1)]
        WQ_KEYS = [('wkvq', 'q', 0), ('wkvq', 'q', 1)]

        def issue_wkvq():
            xkeys = [('xt', k) for k in range(4)]
            for (nm, c0, c1) in (('kv', 0, 256), ('q', 256, 768)):
                for kh in range(2):
                    dma('pool', wkvq[:, kh * 4:(kh + 1) * 4, c0:c1], w_kvq[:, kh * 4:(kh + 1) * 4, c0:c1], xkeys, [('wkvq', nm, kh)], ('wkvq', nm, kh))
                    fold_gain(wkvq[:, :, c0:c1], range(kh * 4, (kh + 1) * 4), [('wkvq', nm, kh)])

        PROJ_B = [2, 3, 4]
        proj_i = [0]
        evac_flip = [0]
        rope_q = []

        def rope_r1a(ps_ap, ps_key, gain_col, dst_ap, dst_key, ct_ap, ct_key, st_ap, st_key):
            c = dict(dst_ap=dst_ap, dst_key=dst_key, st_ap=st_ap, st_key=st_key)
            c['sq'], c['ksq'] = SQ.next()
            c['kgb'], c['kkgb'] = KGB.next()
            c['t1'], c['kt1'] = T1.next()
            sq, kgb, t1 = c['sq'], c['kgb'], c['t1']
            S.op('act', lambda e: e.activation(out=sq, in_=ps_ap, func=AF.Square), r=[ps_key], w=[c['ksq'], ps_key])
            S.op('act', lambda e: e.activation(out=kgb, in_=ps_ap, func=AF.Copy, scale=gain_col), r=[ps_key, 'gqk'], w=[c['kkgb'], ps_key])
            S.op('dve', lambda e: e.scalar_tensor_tensor(out=t1, in0=ps_ap, scalar=gain_col, in1=ct_ap, op0=ALU.mult, op1=ALU.mult),
                 r=[ps_key, 'gqk', ct_key], w=[c['kt1'], ps_key])
            return c

        def rope_r1b(c):
            sq, kgb = c['sq'], c['kgb']
            S.op('pe', lambda e: e.matmul(bank(5), lhsT=swm[:], rhs=kgb, start=True, stop=True), r=['swm', c['kkgb']], w=[bkey(5)], t=0.4)
            S.op('pe', lambda e: e.matmul(bank(6), lhsT=bo[:], rhs=sq, start=True, stop=True), r=['bo', c['ksq']], w=[bkey(6)], t=0.3)

        def rope_r2(c):
            c['t2'], c['kt2'] = T2.next()
            c['sr'], c['ksr'] = SR.next()
            t1, t2, sr, st_ap = c['t1'], c['t2'], c['sr'], c['st_ap']
            S.op('dve', lambda e: e.tensor_tensor(out=t2, in0=bank(5), in1=st_ap, op=ALU.mult), r=[bkey(5), c['st_key']], w=[c['kt2']])
            S.op('act', lambda e: e.activation(out=sr, in_=bank(6), func=AF.Ln, scale=1.0 / 64.0, bias=EPS), r=[bkey(6)], w=[c['ksr']])
            S.op('dve', lambda e: e.tensor_tensor(out=t1, in0=t1, in1=t2, op=ALU.add), r=[c['kt1'], c['kt2']], w=[c['kt1']])

        def rope_r3(c):
            c['ri'], c['kri'] = RI.next()
            t1, sr, ri, dst_ap = c['t1'], c['sr'], c['ri'], c['dst_ap']
            S.op('act', lambda e: e.activation(out=ri, in_=sr, func=AF.Exp, scale=-0.5), r=[c['ksr']], w=[c['kri']])
            S.op('dve', lambda e: e.tensor_tensor(out=dst_ap, in0=t1, in1=ri, op=ALU.mult), r=[c['kt1'], c['kri']], w=[c['dst_key']])

        def rope_push(*args):
            c = rope_r1a(*args)
            if len(rope_q) >= 1:
                rope_r2(rope_q[-1])
            if len(rope_q) >= 2:
                rope_r3(rope_q[-2])
            rope_r1b(c)
            rope_q.append(c)

        def rope_flush():
            if len(rope_q) >= 1:
                rope_r2(rope_q[-1])
            if len(rope_q) >= 2:
                rope_r3(rope_q[-2])
            if len(rope_q) >= 1:
                rope_r3(rope_q[-1])

        def next_proj():
            b = PROJ_B[proj_i[0] % 3]
            proj_i[0] += 1
            return bank(b), bkey(b)

        ginfo = {}

        def stage_a1(g):
            hb, khb = HB.next()
            ginfo[g] = dict(hb=hb, khb=khb)
            for tt in range(4):
                c = g * 4 + tt
                xt, kx = XT.next()
                dma('sp', xt, x[c * P:(c + 1) * P, :], [], [kx], kx)
                S.op('act', lambda e, xt=xt, c=c, hb=hb, tt=tt: e.activation(out=hb[:, tt, :], in_=xt, func=AF.Square, accum_out=ssall[:, c:c + 1]),
                     r=[kx], w=[(khb, tt), ('ss', c)], t=1.0)
                S.op('dve', lambda e, c=c: e.tensor_scalar(out=msall[:, c:c + 1], in0=ssall[:, c:c + 1], scalar1=1.0 / D, scalar2=EPS,
                                                          op0=ALU.mult, op1=ALU.add), r=[('ss', c)], w=[('ms', c)], t=0.08)
                S.op('pool', lambda e, c=c: e.tensor_tensor(out=rsall[:, c:c + 1], in0=msall[:, c:c + 1], in1=negh[:, 0:1], op=ALU.pow),
                     r=[('ms', c), 'negh'], w=[('rs', c)], t=0.5)
                S.op('dve', lambda e, xt=xt, c=c, tt=tt, hb=hb: e.tensor_scalar(out=hb[:, tt, :], in0=xt, scalar1=rsall[:, c:c + 1], scalar2=None, op0=ALU.mult),
                     r=[kx, ('rs', c)], w=[(khb, tt)], t=0.65)

        def stage_a2(g):
            gi = ginfo[g]
            hb, khb = gi['hb'], gi['khb']
            ct, kct = CT.next()
            stt_, kst = STb.next()
            gi.update(ct=ct, kct=kct, st=stt_, kst=kst)
            dma('sp', ct, ctab[:, g * G:(g + 1) * G], [], [kct], kct)
            dma('sp', stt_, stab[:, g * G:(g + 1) * G], [], [kst], kst)
            if g < NG_OWN:
                htg, khtg = HT[:, g], ('HT', g)
            else:
                htg, khtg = HTG.next()
            gi['htg'], gi['khtg'] = htg, khtg
            for tt in range(4):
                b = tt % 2
                pb = bank(b).bitcast(BF16)
                for kc in range(8):
                    S.op('pe', lambda e, pb=pb, tt=tt, kc=kc, hb=hb: e.transpose(
                        out=pb[:, kc * 128:(kc + 1) * 128], in_=hb[:, tt, kc * 128:(kc + 1) * 128], identity=ident[:]),
                        r=[(khb, tt), 'ident'], w=[bkey(b)], t=0.09)
                dst = htg[:, :, tt * 128:(tt + 1) * 128]
                src = pb.rearrange("p (k t) -> p k t", k=8)
                if evac_flip[0] % 4 != 3:
                    S.op('act', lambda e, dst=dst, src=src: e.activation(out=dst, in_=src, func=AF.Copy), r=[bkey(b)], w=[(khtg, tt)])
                else:
                    S.op('dve', lambda e, dst=dst, src=src: e.tensor_copy(out=dst, in_=src), r=[bkey(b)], w=[(khtg, tt)])
                evac_flip[0] += 1

        def stage_a3(g):
            gi = ginfo[g]
            htg, khtg, ct, kct, stt_, kst = gi['htg'], gi['khtg'], gi['ct'], gi['kct'], gi['st'], gi['kst']
            hkeys = [(khtg, kp) for kp in range(4)]
            pk, kbk = next_proj()
            for kc in range(8):
                S.op('pe', lambda e, pk=pk, kc=kc: e.matmul(pk, lhsT=wkvq[:, kc, 0:128], rhs=htg[:, kc, :], start=(kc == 0), stop=(kc == 7)),
                     r=hkeys + WKV_KEYS, w=[kbk])
            rope_push(pk, kbk, gqk_t[:, 1:2], KT[:, g * G:(g + 1) * G], ('KT', g), ct, kct, stt_, kst)
            pv, kbv = next_proj()
            for tt in range(4):
                for kc in range(8):
                    S.op('pe', lambda e, tt=tt, kc=kc, pv=pv: e.matmul(pv[:, tt * 128:(tt + 1) * 128], lhsT=htg[:, kc, tt * 128:(tt + 1) * 128],
                                                                    rhs=wkvq[:, kc, 128:256], start=(kc == 0), stop=(kc == 7)),
                         r=[(khtg, tt)] + WKV_KEYS, w=[kbv], t=0.12)
            vsrc = pv.rearrange("p (t h d) -> p t h d", t=4, h=2)
            S.op('dve', lambda e: e.tensor_copy(out=Vsb[:, g * 4:(g + 1) * 4, 0:64], in_=vsrc[:, :, 0, :]), r=[kbv], w=[('V0', g)])
            S.op('dve', lambda e: e.tensor_copy(out=Vsb[:, g * 4:(g + 1) * 4, 128:192], in_=vsrc[:, :, 1, :]), r=[kbv], w=[('V1', g)])
            if g < NG_OWN:
                for j in range(4):
                    pq, qbk = next_proj()
                    for kc in range(8):
                        S.op('pe', lambda e, pq=pq, kc=kc, j=j: e.matmul(pq, lhsT=wkvq[:, kc, 256 + j * 128:256 + (j + 1) * 128], rhs=htg[:, kc, :],
                                                                      start=(kc == 0), stop=(kc == 7)),
                             r=hkeys + WQ_KEYS, w=[qbk])
                    rope_push(pq, qbk, gqk_t[:, 0:1], QT[:, j, g * G:(g + 1) * G], ('QT', j, g), ct, kct, stt_, kst)

        for s_ in range(n_a_groups + 2):
            if s_ < n_a_groups:
                stage_a1(s_)
            if s_ == 0:
                dma('sp', ngT[:], ngtd, [('xt', 3)], ['ngT'], 'c_ngT')
                dma('sp', gqk_t[:], gqk, [('xt', 3)], ['gqk'], 'c_gqk')
                dma('sp', bm_t[:], bmd, [('xt', 3)], ['bm'], 'c_bm')
                issue_wkvq()
            if 0 <= s_ - 1 < n_a_groups:
                stage_a2(s_ - 1)
            if 0 <= s_ - 2 < n_a_groups:
                stage_a3(s_ - 2)
        rope_flush()
        for cb in (1, 0, 2):
            for kh in range(2):
                dma('pool', wa2[:, kh * 4:(kh + 1) * 4, cb * 512:(cb + 1) * 512], w_a2[:, kh * 4:(kh + 1) * 4, cb * 512:(cb + 1) * 512], [('KT', 0)],
                    [('wa2', cb, kh)], ('wa2', cb, kh))
                fold_gain(wa2[:, :, cb * 512:(cb + 1) * 512], range(kh * 4, (kh + 1) * 4), [('wa2', cb, kh)])

        if debug:
            S.barrier()
            dma('sp', dbg["d_kt"], KT, [], ['dbg1'], 'dbg1')
            dma('sp', dbg["d_v"], arena[:, OFF_V // 2:(OFF_V + 24 * KB) // 2], [], ['dbg2'], 'dbg2')
            dma('sp', dbg["d_qt"], arena[:, OFF_QT // 2:(OFF_QT + 16 * KB) // 2], [], ['dbg3'], 'dbg3')
            dma('sp', dbg["d_ht"], arena[:, OFF_HT // 2:(OFF_HT + 32 * KB) // 2], [], ['dbg4'], 'dbg4')
        S.barrier()

        GROUPS = [[0, 1, 2, 3], [4, 5, 6, 7]]
        dma('sp', cc_srcK, KT[:, 0:S_OWN], [('KT', g) for g in range(4)], ['ccsK'], 'ccsK')
        dma('sp', cc_srcV, Vsb[:, 0:16, :], [('V0', g) for g in range(4)] + [('V1', g) for g in range(4)] + ['vones'], ['ccsV'], 'ccsV')
        S.op('pool', lambda e: e.collective_compute("AllGather", ALU.bypass, replica_groups=GROUPS, ins=[cc_srcK.opt()], outs=[cc_dstK.opt()]),
             r=['ccsK'], w=['ccdK'], dma='ccK', inc=1, t=45.0)
        S.op('pool', lambda e: e.collective_compute("AllGather", ALU.bypass, replica_groups=GROUPS, ins=[cc_srcV.opt()], outs=[cc_dstV.opt()]),
             r=['ccsV'], w=['ccdV'], dma='ccV', inc=1, t=65.0)
        for rk in range(4):
            dma('sp', KT[:, rk * S_OWN:(rk + 1) * S_OWN], cc_dstK[rk * P:(rk + 1) * P, :], ['ccdK', 'ccsK'],
                [('KT', rk * 4 + g) for g in range(4)], ('ldK', rk))
        for rk in range(4):
            dma('sp', Vsb[:, rk * 16:(rk + 1) * 16, :], cc_dstV[rk * P:(rk + 1) * P, :].rearrange("p (k c) -> p k c", c=192), ['ccdV', 'ccsV'],
                [('V0', rk * 4 + g) for g in range(4)] + [('V1', rk * 4 + g) for g in range(4)] + ['vones'], ('ldV', rk))

        NGB = min(NG_OWN, n_a_groups) if max_phase >= 2 else 0
        WB = Bump(arena, OFF_ATTN, WA_END)
        wst = WB.alloc(2 * KB, BF16).rearrange("p (g i) -> p g i", g=8)
        lng = WB.alloc(2 * KB, F32)
        lnb = WB.alloc(2 * KB, F32)
        bsp = WB.alloc(8 * KB, F32).rearrange("p (c t) -> p c t", c=4)
        GV = Rot('gv', [WB.alloc(2 * KB, F32) for _ in range(2)])
        VN = Rot('vn', [WB.alloc(2 * KB, F32) for _ in range(2)])
        VTOK = Rot('vtok', [WB.alloc(4 * KB, BF16).rearrange("p (t c) -> p t c", t=4) for _ in range(2)])
        USB = Rot('usb', [WB.alloc(8 * KB, F32).rearrange("p (c t) -> p c t", c=4) for _ in range(2)])
        SGB = Rot('sgb', [WB.alloc(2 * KB, F32) for _ in range(2)])
        TA = Rot('ta', [WB.alloc(2 * KB, F32) for _ in range(2)])
        TB = Rot('tb', [WB.alloc(2 * KB, F32) for _ in range(2)])
        WA2_U = [('wa2', 0, 0), ('wa2', 0, 1)]
        WA2_V = [('wa2', 1, 0), ('wa2', 1, 1)]
        WA2_G = [('wa2', 2, 0), ('wa2', 2, 1)]
        dma('pool', wst, wsT, [], ['wst'], 'c_wst')
        dma('sp', lng, lngv.partition_broadcast(P), [], ['lng'], 'c_lng')
        dma('sp', lnb, lnbv.partition_broadcast(P), [], ['lnb'], 'c_lnb')
        dma('sp', bsp, bspd, [], ['bsp'], 'c_bsp')
        VIN = Rot('vinb', [bank(0), bank(1)])
        UB = Rot('ub', [bank(2), bank(3)])
        for g in range(NGB):
            ht = HT[:, g]
            hk = [('HT', g)]
            vtok, kvtok = VTOK.next()
            usb, kusb = USB.next()
            for tt in range(4):
                c = g * 4 + tt
                pv, _ = VIN.next()
                bk = ('ps', (VIN.i - 1) % 2)
                for kc in range(8):
                    S.op('pe', lambda e, pv=pv, kc=kc, tt=tt, ht=ht: e.matmul(pv, lhsT=ht[:, kc, tt * 128:(tt + 1) * 128], rhs=wa2[:, kc, 512:1024],
                                                                           start=(kc == 0), stop=(kc == 7)), r=hk + WA2_V, w=[bk])
                gv, kgv = GV.next()
                vn, kvn = VN.next()
                S.op('act', lambda e, gv=gv, pv=pv: e.activation(out=gv, in_=pv, func=AF.Gelu_apprx_tanh), r=[bk], w=[kgv])
                S.op('dve', lambda e, gv=gv, c=c: e.bn_stats(out=lnst[:, c * 6:(c + 1) * 6], in_=gv), r=[kgv], w=[('lnst', c)])
                S.op('dve', lambda e, c=c: e.bn_aggr(out=lnmv[:, c * 2:(c + 1) * 2], in_=lnst[:, c * 6:(c + 1) * 6]), r=[('lnst', c)], w=[('lnmv', c)], t=0.08)
                S.op('dve', lambda e, c=c: e.tensor_scalar(out=lnve[:, c:c + 1], in0=lnmv[:, c * 2 + 1:c * 2 + 2], scalar1=EPS, scalar2=None, op0=ALU.add),
                     r=[('lnmv', c)], w=[('lnve', c)], t=0.08)
                S.op('pool', lambda e, c=c: e.tensor_tensor(out=lnrs[:, c:c + 1], in0=lnve[:, c:c + 1], in1=negh[:, 0:1], op=ALU.pow),
                     r=[('lnve', c), 'negh'], w=[('lnrs', c)], t=0.5)
                S.op('dve', lambda e, gv=gv, vn=vn, c=c: e.tensor_scalar(out=vn, in0=gv, scalar1=lnmv[:, c * 2:c * 2 + 1], scalar2=lnrs[:, c:c + 1],
                                                                       op0=ALU.subtract, op1=ALU.mult), r=[kgv, ('lnmv', c), ('lnrs', c)], w=[kvn])
                S.op('pool', lambda e, vn=vn: e.tensor_tensor(out=vn, in0=vn, in1=lng, op=ALU.mult), r=[kvn, 'lng'], w=[kvn])
                S.op('pool', lambda e, vn=vn, vtok=vtok, tt=tt: e.tensor_tensor(out=vtok[:, tt, :], in0=vn, in1=lnb, op=ALU.add), r=[kvn, 'lnb'], w=[(kvtok, tt)])
            for ct_ in range(4):
                pu, _ = UB.next()
                bk = ('ps', 2 + (UB.i - 1) % 2)
                for kc in range(8):
                    S.op('pe', lambda e, pu=pu, kc=kc, ct_=ct_, ht=ht: e.matmul(pu, lhsT=wa2[:, kc, ct_ * 128:(ct_ + 1) * 128], rhs=ht[:, kc, :],
                                                                             start=(kc == 0), stop=(kc == 7)), r=hk + WA2_U, w=[bk])
                S.op('act', lambda e, pu=pu, ct_=ct_, usb=usb: e.activation(out=usb[:, ct_, :], in_=pu, func=AF.Gelu_apprx_tanh), r=[bk], w=[(kusb, ct_)])
            for pr in range(4):
                for tt in range(4):
                    for half in range(2):
                        grp = 2 * pr + half
                        S.op('pe', lambda e, pr=pr, tt=tt, half=half, grp=grp, vtok=vtok: e.matmul(
                            bank(4 + pr)[half * 64:(half + 1) * 64, tt * 128:(tt + 1) * 128], lhsT=vtok[:, tt, grp * 64:(grp + 1) * 64],
                            rhs=wst[:, grp, :], start=True, stop=True), r=[(kvtok, tt), 'wst'], w=[bkey(4 + pr)], t=0.06)
            for ct_ in range(4):
                pgb, _ = VIN.next()
                bk = ('ps', (VIN.i - 1) % 2)
                for kc in range(8):
                    S.op('pe', lambda e, pgb=pgb, kc=kc, ct_=ct_, ht=ht: e.matmul(pgb, lhsT=wa2[:, kc, 1024 + ct_ * 128:1024 + (ct_ + 1) * 128], rhs=ht[:, kc, :],
                                                                               start=(kc == 0), stop=(kc == 7)), r=hk + WA2_G, w=[bk])
                sgb, ksgb = SGB.next()
                ta, kta = TA.next()
                tb, ktb = TB.next()
                S.op('act', lambda e, sgb=sgb, pgb=pgb: e.activation(out=sgb, in_=pgb, func=AF.Silu), r=[bk], w=[ksgb])
                S.op('dve', lambda e, ta=ta, ct_=ct_: e.tensor_tensor(out=ta, in0=bank(4 + ct_), in1=bsp[:, ct_, :], op=ALU.add), r=[bkey(4 + ct_), 'bsp'], w=[kta])
                S.op('pool', lambda e, tb=tb, sgb=sgb, ct_=ct_, usb=usb: e.tensor_tensor(out=tb, in0=usb[:, ct_, :], in1=sgb, op=ALU.mult), r=[(kusb, ct_), ksgb], w=[ktb])
                S.op('dve', lambda e, ta=ta, tb=tb, ct_=ct_, g=g: e.tensor_tensor(out=GM[:, ct_, g * G:(g + 1) * G], in0=ta, in1=tb, op=ALU.mult), r=[kta, ktb], w=[('GM', ct_, g)])
        if debug:
            S.barrier()
            dma('sp', dbg["d_gm"], arena[:, OFF_GM // 2:(OFF_GM + 16 * KB) // 2], [], ['dbg5'], 'dbg5')
        S.barrier()

        S.reorder = False
        WC = Bump(arena, OFF_FLEX, ARENA)
        wga = WC.alloc(8 * KB, BF16).rearrange("p (k n) -> p k n", k=8)
        wpa = WC.alloc(8 * KB, BF16).rearrange("p (k n) -> p k n", k=4)
        wg = WC.alloc(32 * KB, BF16).rearrange("p (k n) -> p k n", k=8)
        wout = WC.alloc(16 * KB, BF16).rearrange("p (k n) -> p k n", k=8)
        WC_WORK = WC.off
        PT = Rot('pt', [WC.alloc(2 * KB, BF16) for _ in range(4)])
        OSB = [WC.alloc(2 * KB, F32) for _ in range(2)]
        _rs = WC.alloc(2 * KB, F32)
        _rs2 = WC.alloc(2 * KB, F32)
        RS = [_rs, _rs]
        RS2 = [_rs2, _rs2]
        for kc in range(0, 8, 2):
            dma('pool', wg[:, kc:kc + 2, :], w_g[:, kc:kc + 2, :], [], [('wg', kc)], ('wg', kc))
            fold_gain(wg, [kc, kc + 1], [('wg', kc)])
        WG_KEYS = [('wg', kc) for kc in range(0, 8, 2)]
        for kc in range(0, 8, 4):
            dma('pool', wout[:, kc:kc + 4, :], w_o[:, kc:kc + 4, :], [], [('wout', kc)], ('wout', kc))
        WOUT_KEYS = [('wout', 0), ('wout', 4)]
        dma('pool', wga, w_ga, [], ['wga'], 'c_wga')
        fold_gain(wga, range(8), ['wga'])
        dma('pool', wpa, w_pa, [], ['wpa'], 'c_wpa')

        SC = 0.125
        OBK = 6
        passes = [(qg, j) for qg in range(NG_OWN if max_phase >= 3 else 0) for j in range(4)]
        NT = len(passes) * 64

        def qkmm(t):
            qg, j = passes[t // 64]
            kt = t % 64
            sb_ = t % 3
            qA = QT[0:64, j, qg * G:(qg + 1) * G]
            qB = QT[64:128, j, qg * G:(qg + 1) * G]
            rk = [('KT', kt // 4), ('QT', j, qg)]
            S.op('pe', lambda e: e.matmul(PS[sb_][:, 0:512], lhsT=KT[0:64, kt * 128:(kt + 1) * 128], rhs=qA, start=True, stop=True),
                 r=rk, w=[bkey(2 * sb_)])
            S.op('pe', lambda e: e.matmul(PS[sb_][:, 512:1024], lhsT=KT[64:128, kt * 128:(kt + 1) * 128], rhs=qB, start=True, stop=True),
                 r=rk, w=[bkey(2 * sb_ + 1)])

        def expo(t):
            sb_ = t % 3
            pt, kpt = PT.next()
            S.op('act', lambda e: e.activation(out=pt, in_=PS[sb_][:, :], func=AF.Exp, scale=SC, bias=-12.0),
                 r=[bkey(2 * sb_), bkey(2 * sb_ + 1)], w=[kpt], t=1.0)
            return pt, kpt

        def normalise(qg, j, last=False):
            S.op('dve', lambda e: e.tensor_copy(out=OSB[0], in_=bank(OBK)), r=[bkey(OBK)], w=['osb0'])
            S.op('dve', lambda e: e.tensor_copy(out=OSB[1], in_=bank(OBK + 1)), r=[bkey(OBK + 1)], w=['osb1'])
            S.op('dve', lambda e: e.reciprocal(out=RS[0][64:128, :], in_=OSB[0][64:128, :]), r=['osb0'], w=['rs0'])
            if last:
                S.op('act', lambda e: e.activation(out=RS[1][0:64, :], in_=OSB[1][0:64, :], func=AF.Ln), r=['osb1'], w=['rs1'])
                S.op('act', lambda e: e.activation(out=RS[1][0:64, :], in_=RS[1][0:64, :], func=AF.Exp, scale=-1.0), r=['rs1'], w=['rs1'])
            else:
                S.op('dve', lambda e: e.reciprocal(out=RS[1][0:64, :], in_=OSB[1][0:64, :]), r=['osb1'], w=['rs1'])
            S.op('dve', lambda e: e.tensor_copy(out=RS2[0][0:64, :], in_=RS[0][64:128, :]), r=['rs0'], w=['rs20'])
            S.op('dve', lambda e: e.tensor_copy(out=RS2[1][64:128, :], in_=RS[1][0:64, :]), r=['rs1'], w=['rs21'])
            S.op('dve', lambda e: e.tensor_tensor(out=ATT[0:64, j, qg * G:(qg + 1) * G], in0=OSB[0][0:64, :], in1=RS2[0][0:64, :], op=ALU.mult),
                 r=['osb0', 'rs20'], w=[('ATT0', j, qg)])
            S.op('dve', lambda e: e.tensor_tensor(out=ATT[64:128, j, qg * G:(qg + 1) * G], in0=OSB[1][64:128, :], in1=RS2[1][64:128, :], op=ALU.mult),
                 r=['osb1', 'rs21'], w=[('ATT1', j, qg)])

        def pvmm(t, pt, kpt):
            qg, j = passes[t // 64]
            kt = t % 64
            vk = [('V0', kt // 4), ('V1', kt // 4), 'vones']
            S.op('pe', lambda e: e.matmul(bank(OBK), lhsT=Vsb[:, kt, 0:128], rhs=pt[:, 0:512], start=(kt == 0), stop=(kt == 63)),
                 r=[kpt] + vk, w=[bkey(OBK)])
            S.op('pe', lambda e: e.matmul(bank(OBK + 1), lhsT=Vsb[:, kt, 64:192], rhs=pt[:, 512:1024], start=(kt == 0), stop=(kt == 63)),
                 r=[kpt] + vk, w=[bkey(OBK + 1)])
            if kt == 63:
                normalise(qg, j, last=(t == NT - 1))

        if NT:
            qkmm(0)
            qkmm(1)
        for t in range(0, NT, 2):
            p0 = expo(t)
            p1 = expo(t + 1)
            if t + 2 < NT:
                qkmm(t + 2)
                qkmm(t + 3)
            pvmm(t, *p0)
            pvmm(t + 1, *p1)
        hoisted = {}
        if NT and max_phase >= 4:
            for j in range(2):
                for kc in range(8):
                    S.op('pe', lambda e, kc=kc, j=j: e.matmul(bank(j), lhsT=wga[:, kc, j * 128:(j + 1) * 128], rhs=HT[:, 0][:, kc, :],
                                                           start=(kc == 0), stop=(kc == 7)), r=[('HT', 0), 'wga'], w=[bkey(j)])
                hoisted[(0, j)] = True
        if debug:
            S.barrier()
            dma('sp', dbg["d_attn"], arena[:, OFF_ATTN // 2:(OFF_ATTN + 16 * KB) // 2], [], ['dbg6'], 'dbg6')
        S.barrier()

        S.reorder = True
        WD = Bump(arena, OFF_KT, OFF_HT)
        wpg = WD.alloc(8 * KB, BF16).rearrange("p (k n) -> p k n", k=4)
        fgt = WD.alloc(4 * KB, F32)
        WD2 = Bump(arena, WC_WORK, ARENA)
        XC = Rot('xc', [WD2.alloc(4 * KB, F32) for _ in range(2)] + [WD.alloc(4 * KB, F32)])
        SGA = Rot('sga', [WD.alloc(2 * KB, F32) for _ in range(2)])
        AT = Rot('aT', [WD.alloc(4 * KB, BF16).rearrange("p (j t) -> p j t", j=4) for _ in range(2)])
        G0 = Rot('g0', [WD.alloc(2 * KB, F32) for _ in range(2)])
        G1 = Rot('g1', [WD.alloc(2 * KB, F32) for _ in range(2)])
        TC = Rot('tc', [WD.alloc(2 * KB, F32) for _ in range(2)])
        TD = Rot('td', [WD.alloc(2 * KB, F32) for _ in range(2)])
        YT = Rot('yT', [WD.alloc(8 * KB, BF16).rearrange("p (k t) -> p k t", k=8) for _ in range(1)])
        RR = Rot('rr', [WD2.alloc(4 * KB, F32) for _ in range(2)] + [WD.alloc(4 * KB, F32)])
        dma('pool', wpg, w_pg, [], ['wpg'], 'c_wpg')
        dma('sp', fgt, fgv.partition_broadcast(P), [], ['fgt'], 'c_fgt')
        GAB = Rot('gab', [bank(0), bank(1)])
        tails = []

        def emit_tail(rr, krr, c):
            S.op('dve', lambda e: e.scalar_tensor_tensor(out=rr, in0=rr, scalar=frs[:, c:c + 1], in1=fgt, op0=ALU.mult, op1=ALU.mult),
                 r=[(krr, 0), (krr, 1), ('frs', c), 'fgt'], w=[(krr, 0), (krr, 1)], t=1.3)
            dma('sp', out[c * P:(c + 1) * P, :], rr, [(krr, 0), (krr, 1)], [('out', c)], ('o', c % 3))
        OB = Rot('ob', [bank(6), bank(7)])
        for g in range(NG_OWN if max_phase >= 4 else 0):
            ht = HT[:, g]
            hk = [('HT', g)]
            aT, kaT = AT.next()
            yT, kyT = YT.next()
            for j in range(4):
                pga, _ = GAB.next()
                bk = ('ps', (GAB.i - 1) % 2)
                for kc in range(8):
                    if (g, j) in hoisted:
                        continue
                    S.op('pe', lambda e, pga=pga, kc=kc, j=j, ht=ht: e.matmul(pga, lhsT=wga[:, kc, j * 128:(j + 1) * 128], rhs=ht[:, kc, :],
                                                                           start=(kc == 0), stop=(kc == 7)), r=hk + ['wga'], w=[bk])
                sga, ksga = SGA.next()
                S.op('act', lambda e, sga=sga, pga=pga: e.activation(out=sga, in_=pga, func=AF.Silu), r=[bk], w=[ksga])
                S.op('dve', lambda e, sga=sga, j=j, g=g, aT=aT: e.tensor_tensor(out=aT[:, j, :], in0=sga, in1=ATT[:, j, g * G:(g + 1) * G], op=ALU.mult),
                     r=[ksga, ('ATT0', j, g), ('ATT1', j, g)], w=[(kaT, j)])
            for ot in range(8):
                g0, kg0 = G0.next()
                g1, kg1 = G1.next()
                tc, ktc = TC.next()
                td, ktd = TD.next()
                for kc in range(8):
                    S.op('pe', lambda e, ot=ot, kc=kc, ht=ht: e.matmul(bank(4), lhsT=wg[:, kc, ot * 128:(ot + 1) * 128], rhs=ht[:, kc, :], start=(kc == 0), stop=(kc == 7)),
                         r=hk + WG_KEYS, w=[bkey(4)])
                S.op('act', lambda e, g0=g0, ot=ot: e.activation(out=g0, in_=bank(4), func=AF.Sigmoid, bias=bm_t[:, ot:ot + 1]), r=[bkey(4), 'bm'], w=[kg0])
                for kc in range(8):
                    S.op('pe', lambda e, ot=ot, kc=kc, ht=ht: e.matmul(bank(5), lhsT=wg[:, kc, 1024 + ot * 128:1024 + (ot + 1) * 128], rhs=ht[:, kc, :],
                                                                    start=(kc == 0), stop=(kc == 7)), r=hk + WG_KEYS, w=[bkey(5)])
                S.op('act', lambda e, g1=g1, ot=ot: e.activation(out=g1, in_=bank(5), func=AF.Sigmoid, bias=bm_t[:, 8 + ot:9 + ot]), r=[bkey(5), 'bm'], w=[kg1])
                for j in range(4):
                    S.op('pe', lambda e, ot=ot, j=j, aT=aT: e.matmul(bank(2), lhsT=wpa[:, j, ot * 128:(ot + 1) * 128], rhs=aT[:, j, :], start=(j == 0), stop=(j == 3)),
                         r=['wpa', (kaT, j)], w=[bkey(2)])
                S.op('dve', lambda e, tc=tc, g0=g0: e.tensor_tensor(out=tc, in0=bank(2), in1=g0, op=ALU.mult), r=[bkey(2), kg0], w=[ktc])
                for c4 in range(4):
                    S.op('pe', lambda e, ot=ot, c4=c4, g=g: e.matmul(bank(3), lhsT=wpg[:, c4, ot * 128:(ot + 1) * 128], rhs=GM[:, c4, g * G:(g + 1) * G],
                                                                  start=(c4 == 0), stop=(c4 == 3)), r=['wpg', ('GM', c4, g)], w=[bkey(3)])
                S.op('dve', lambda e, td=td, g1=g1: e.tensor_tensor(out=td, in0=bank(3), in1=g1, op=ALU.mult), r=[bkey(3), kg1], w=[ktd])
                S.op('pool', lambda e, tc=tc, td=td, ot=ot, yT=yT: e.tensor_tensor(out=yT[:, ot, :], in0=tc, in1=td, op=ALU.add), r=[ktc, ktd], w=[(kyT, ot)])
            ykeys = [(kyT, ot) for ot in range(8)]
            for tt in range(4):
                c = g * 4 + tt
                xc, kxc = XC.next()
                rr, krr = RR.next()
                dma('sp', xc, x[c * P:(c + 1) * P, :], [], [kxc], kxc)
                for half in range(2):
                    po, _ = OB.next()
                    bk = ('ps', 6 + (OB.i - 1) % 2)
                    for ot in range(8):
                        S.op('pe', lambda e, po=po, ot=ot, tt=tt, half=half, yT=yT: e.matmul(po, lhsT=yT[:, ot, tt * 128:(tt + 1) * 128],
                                                                                          rhs=wout[:, ot, half * 512:(half + 1) * 512], start=(ot == 0), stop=(ot == 7)),
                             r=ykeys + WOUT_KEYS, w=[bk])
                    S.op('dve', lambda e, po=po, half=half, rr=rr, xc=xc: e.tensor_tensor(out=rr[:, half * 512:(half + 1) * 512], in0=po,
                                                                                       in1=xc[:, half * 512:(half + 1) * 512], op=ALU.add),
                         r=[bk, kxc], w=[(krr, half)])
                S.op('act', lambda e, rr=rr, c=c, xc=xc: e.activation(out=xc, in_=rr, func=AF.Square, accum_out=fss[:, c:c + 1]),
                     r=[(krr, 0), (krr, 1)], w=[kxc, ('fss', c)], t=1.0)
                S.op('dve', lambda e, c=c: e.tensor_scalar(out=fms[:, c:c + 1], in0=fss[:, c:c + 1], scalar1=1.0 / D, scalar2=EPS, op0=ALU.mult, op1=ALU.add),
                     r=[('fss', c)], w=[('fms', c)], t=0.08)
                S.op('pool', lambda e, c=c: e.tensor_tensor(out=frs[:, c:c + 1], in0=fms[:, c:c + 1], in1=negh[:, 0:1], op=ALU.pow),
                     r=[('fms', c), 'negh'], w=[('frs', c)], t=0.5)
                tails.append((rr, krr, c))
                if len(tails) >= 2:
                    emit_tail(*tails.pop(0))
        while tails:
            emit_tail(*tails.pop(0))
        S.barrier()
        S.emit(nc, st)
    return nc, S


def _rope_tables(r):
    pos = np.arange(S_OWN) + r * S_OWN
    row = (pos // 64).astype(np.float64)
    col = (pos % 64).astype(np.float64)
    inv = 10000.0 ** (-(np.arange(0, 32, 2, dtype=np.float64) / 32.0))
    ct = np.zeros((P, S_OWN), np.float32)
    sn = np.zeros((P, S_OWN), np.float32)
    for p in range(P):
        d = p % 64
        idx = row if d < 32 else col
        dd = d % 32
        f = dd % 16
        ang = idx * inv[f]
        ct[p] = np.cos(ang)
        sn[p] = -np.sin(ang) if dd < 16 else np.sin(ang)
    return ct, sn


def _tile_rows(w, nk):
    return np.ascontiguousarray(w.reshape(nk, P, w.shape[1]).transpose(1, 0, 2))


def prepare_inputs(x, norm_gain, w_in, q_gain, k_gain, w_proj_attn, ln_v_gain, ln_v_bias,
                   w_spatial, b_spatial, w_proj_gmlp, b_merge, w_out, final_gain):
    f = lambda a: np.ascontiguousarray(np.asarray(a, dtype=np.float32))
    x = f(x); w_in = f(w_in)[0]
    wq, wk, wv, wga, wu, wvi, wgb, wgm = np.split(w_in, np.cumsum([512, 128, 128, 512, 512, 512, 512])[:], axis=1)
    pair_cols = np.concatenate([np.r_[j * 64:(j + 1) * 64, (4 + j) * 64:(5 + j) * 64] for j in range(4)])
    shared = {
        "w_kvq": _tile_rows(np.concatenate([wk, wv, wq[:, pair_cols]], axis=1), 8),
        "w_a2": _tile_rows(np.concatenate([wu, wvi, wgb], axis=1), 8),
        "w_ga": _tile_rows(wga[:, pair_cols], 8),
        "w_g": _tile_rows(wgm, 8),
        "w_pa": _tile_rows(f(w_proj_attn)[0][pair_cols, :], 4),
        "w_pg": _tile_rows(f(w_proj_gmlp)[0], 4),
        "w_o": _tile_rows(f(w_out)[0], 8),
        "wsT": np.ascontiguousarray(f(w_spatial)[0].transpose(2, 0, 1)),
        "ng": f(norm_gain)[0],
        "ng_t": np.ascontiguousarray(f(norm_gain)[0].reshape(8, P).T),
        "gqk": np.ascontiguousarray(np.stack([np.tile(f(q_gain)[0], 2), np.tile(f(k_gain)[0], 2)], axis=1)),
        "gq_row": f(q_gain)[0],
        "gk_row": f(k_gain)[0],
        "lng": f(ln_v_gain)[0],
        "lnb": f(ln_v_bias)[0],
        "bm": np.ascontiguousarray(f(b_merge)[0].reshape(16, P).T),
        "fg": f(final_gain),
        "ident": np.eye(P, dtype=np.float32),
    }
    bs = f(b_spatial)[0]
    bsp = np.zeros((P, 4, 4, 128), np.float32)
    for pr in range(4):
        bsp[0:64, pr, :, :] = bs[2 * pr][None, None, :]
        bsp[64:128, pr, :, :] = bs[2 * pr + 1][None, None, :]
    shared["bsp"] = np.ascontiguousarray(bsp.reshape(P, 4, 512))
    partner = np.array([(p // 32) * 32 + ((p % 32) + 16) % 32 for p in range(P)])
    swm = np.zeros((P, P), np.float32)
    swm[partner, np.arange(P)] = 1.0
    shared["swm"] = swm
    bo = np.zeros((P, P), np.float32)
    bo[0:64, 0:64] = 1.0
    bo[64:128, 64:128] = 1.0
    shared["bo"] = bo
    tabs = [_rope_tables(r) for r in range(4)]
    in_maps = []
    for c in range(8):
        b, r = c // 4, c % 4
        m = dict(shared)
        m["x"] = np.ascontiguousarray(x[b, r * S_OWN:(r + 1) * S_OWN])
        m["ctab"], m["stab"] = tabs[r]
        in_maps.append(m)
    return in_maps


_CACHE = {}


def kernel(**inputs):
    in_maps = prepare_inputs(**inputs)
    if "nc" not in _CACHE:
        _CACHE["nc"] = build_program(False)[0]
    nc = _CACHE["nc"]
    res = run_bass_kernel_spmd(nc, in_maps, core_ids=list(range(8)))
    outp = np.zeros((2, S_ALL, D), np.float32)
    for c in range(8):
        b, r = c // 4, c % 4
        outp[b, r * S_OWN:(r + 1) * S_OWN, :] = np.asarray(res.results[c]["out"], dtype=np.float32)
    return outp
```
